# Optimizing a Trainium2 kernel written in Bass

```python
import jax
import jax.numpy as jnp
from jax import lax
import numpy as np

D_MODEL = 4096
BATCH = 8
SEQ = 2048
DEPTH = 1

HEAD_DIM = 128
MIX_WIDTH = D_MODEL
ATTN_WIDTH = MIX_WIDTH // 2
REC_WIDTH = MIX_WIDTH - ATTN_WIDTH
ATTN_HEADS = ATTN_WIDTH // HEAD_DIM
REC_HEADS = REC_WIDTH // HEAD_DIM
REC_KEY_DIM = 128
REC_KEY_WIDTH = REC_HEADS * REC_KEY_DIM
ROPE_THETA = 500000.0
ROPE_DIM = HEAD_DIM // 4
MOBA_BLOCK = 256
MOBA_TOPK = 3
MOBA_QCHUNK = 16
REC_CHUNK = 64
NORM_EPS = 1e-6
NEG_INF = -1e30
SPLITS = (ATTN_WIDTH, ATTN_WIDTH, ATTN_WIDTH, ATTN_WIDTH, REC_KEY_WIDTH, REC_KEY_WIDTH, REC_WIDTH, REC_WIDTH)
SPLIT_POINTS = tuple(int(s) for s in np.cumsum(SPLITS)[:-1])
IN_WIDTH = sum(SPLITS)

kernel_name = 'hybrid_moba_hgrn2_layer'


def rms_norm(x, w):
    xf = x.astype(jnp.float32)
    y = xf * lax.rsqrt(jnp.mean(xf * xf, axis=-1, keepdims=True) + NORM_EPS)
    return (y * w.astype(jnp.float32)).astype(x.dtype)


def to_heads(a, n_heads):
    b, t, _ = a.shape
    return a.reshape(b, t, n_heads, -1).transpose(0, 2, 1, 3)


def from_heads(a):
    b, h, t, d = a.shape
    return a.transpose(0, 2, 1, 3).reshape(b, t, h * d)


def partial_rope(x, pos):
    half = ROPE_DIM // 2
    inv_freq = jnp.power(ROPE_THETA, -jnp.arange(half, dtype=jnp.float32) / half)
    ang = pos.astype(jnp.float32)[:, None] * inv_freq[None, :]
    cos = jnp.cos(ang).astype(x.dtype)
    sin = jnp.sin(ang).astype(x.dtype)
    x1 = x[..., :half]
    x2 = x[..., half:ROPE_DIM]
    return jnp.concatenate([x1 * cos - x2 * sin, x2 * cos + x1 * sin, x[..., ROPE_DIM:]], axis=-1)


def moba_attention(q, k, v):
    b, h, t, d = q.shape
    n_blocks = -(-t // MOBA_BLOCK)
    t_pad = n_blocks * MOBA_BLOCK
    pad = ((0, 0), (0, 0), (0, t_pad - t), (0, 0))
    q, k, v = jnp.pad(q, pad), jnp.pad(k, pad), jnp.pad(v, pad)
    k_blk = k.reshape(b, h, n_blocks, MOBA_BLOCK, d)
    v_blk = v.reshape(b, h, n_blocks, MOBA_BLOCK, d)
    k_mean = jnp.mean(k_blk.astype(jnp.float32), axis=3).astype(q.dtype)
    top_k = min(MOBA_TOPK, n_blocks)
    scale = d ** -0.5
    n_chunks = t_pad // MOBA_QCHUNK
    q_chunks = jnp.moveaxis(q.reshape(b, h, n_chunks, MOBA_QCHUNK, d), 2, 0)
    b_idx = jnp.arange(b)[:, None, None, None]
    h_idx = jnp.arange(h)[None, :, None, None]
    blk_ids = jnp.arange(n_blocks)
    sel_slots = jnp.arange(top_k)

    def chunk_step(args):
        c, q_c = args
        q_pos = c * MOBA_QCHUNK + jnp.arange(MOBA_QCHUNK)
        own = (c * MOBA_QCHUNK) // MOBA_BLOCK
        gate = jnp.einsum('bhqd,bhnd->bhqn', q_c, k_mean).astype(jnp.float32)
        gate = jnp.where(blk_ids < own, gate, NEG_INF)
        _, sel = lax.top_k(gate, top_k)
        k_sel = k_blk[b_idx, h_idx, sel]
        v_sel = v_blk[b_idx, h_idx, sel]
        s_past = jnp.einsum('bhqd,bhqnkd->bhqnk', q_c, k_sel).astype(jnp.float32) * scale
        s_past = jnp.where((sel_slots < own)[:, None], s_past, NEG_INF)
        k_own = lax.dynamic_index_in_dim(k_blk, own, axis=2, keepdims=False)
        v_own = lax.dynamic_index_in_dim(v_blk, own, axis=2, keepdims=False)
        s_own = jnp.einsum('bhqd,bhkd->bhqk', q_c, k_own).astype(jnp.float32) * scale
        k_pos = own * MOBA_BLOCK + jnp.arange(MOBA_BLOCK)
        s_own = jnp.where(k_pos[None, :] <= q_pos[:, None], s_own, NEG_INF)
        scores = jnp.concatenate([s_past.reshape(b, h, MOBA_QCHUNK, top_k * MOBA_BLOCK), s_own], axis=-1)
        p = jax.nn.softmax(scores, axis=-1).astype(v.dtype)
        p_past = p[..., :top_k * MOBA_BLOCK].reshape(b, h, MOBA_QCHUNK, top_k, MOBA_BLOCK)
        p_own = p[..., top_k * MOBA_BLOCK:]
        return (jnp.einsum('bhqnk,bhqnkd->bhqd', p_past, v_sel)
                + jnp.einsum('bhqk,bhkd->bhqd', p_own, v_own))

    out = lax.map(chunk_step, (jnp.arange(n_chunks), q_chunks))
    out = jnp.moveaxis(out, 0, 2).reshape(b, h, t_pad, d)
    return out[:, :, :t]


def hgrn2_recurrence(q, k, v, log_f):
    b, h, t, dk = q.shape
    dv = v.shape[-1]
    n = t // REC_CHUNK

    def chunks(a):
        return a.reshape(b, h, n, REC_CHUNK, a.shape[-1])

    q, k, v, log_f = chunks(q), chunks(k), chunks(v), chunks(log_f)
    cum = jnp.cumsum(log_f, axis=3)
    cum_last = cum[..., -1:, :]
    cum_mid = cum[..., REC_CHUNK // 2 - 1:REC_CHUNK // 2, :]
    q_in = q * jnp.exp(cum - cum_mid)
    k_in = k * jnp.exp(cum_mid - cum)
    scores = jnp.einsum('bhncd,bhnsd->bhncs', q_in, k_in)
    causal = jnp.arange(REC_CHUNK)[:, None] >= jnp.arange(REC_CHUNK)[None, :]
    scores = jnp.where(causal, scores, 0.0)
    o_intra = jnp.einsum('bhncs,bhnse->bhnce', scores, v)
    q_inter = q * jnp.exp(cum)
    k_state = k * jnp.exp(cum_last - cum)
    decay_chunk = jnp.exp(cum_last[..., 0, :])

    def scan_fn(state, xs):
        qi, ks, vc, dc = xs
        o = jnp.einsum('bhcd,bhde->bhce', qi, state)
        state = dc[..., :, None] * state + jnp.einsum('bhcd,bhce->bhde', ks, vc)
        return state, o

    xs = (jnp.moveaxis(q_inter, 2, 0), jnp.moveaxis(k_state, 2, 0),
          jnp.moveaxis(v, 2, 0), jnp.moveaxis(decay_chunk, 2, 0))
    state0 = jnp.zeros((b, h, dk, dv), jnp.float32)
    _, o_inter = lax.scan(scan_fn, state0, xs)
    o = o_intra + jnp.moveaxis(o_inter, 0, 2)
    return o.reshape(b, h, t, dv)


def setup_inputs(seed: int = 0) -> dict:
    key = jax.random.key(seed)
    ks = jax.random.split(key, 7)
    x = jax.random.normal(ks[0], (BATCH, SEQ, D_MODEL), jnp.float32)
    norm_w = 1.0 + 0.02 * jax.random.normal(ks[1], (DEPTH, D_MODEL), jnp.float32)
    w_in = jax.random.normal(ks[2], (DEPTH, D_MODEL, IN_WIDTH), jnp.float32) * D_MODEL ** -0.5
    rec_lower_bound_logits = 0.1 * jax.random.normal(ks[3], (DEPTH + 1, REC_KEY_WIDTH), jnp.float32)
    rec_out_norm_w = 1.0 + 0.02 * jax.random.normal(ks[4], (DEPTH, REC_WIDTH), jnp.float32)
    w_out = jax.random.normal(ks[5], (DEPTH, MIX_WIDTH, D_MODEL), jnp.float32) * MIX_WIDTH ** -0.5
    final_norm_w = 1.0 + 0.02 * jax.random.normal(ks[6], (D_MODEL,), jnp.float32)
    return {'x': x, 'norm_w': norm_w, 'w_in': w_in,
            'rec_lower_bound_logits': rec_lower_bound_logits,
            'rec_out_norm_w': rec_out_norm_w, 'w_out': w_out,
            'final_norm_w': final_norm_w}


def reference(x, norm_w, w_in, rec_lower_bound_logits, rec_out_norm_w, w_out, final_norm_w):
    t = x.shape[1]
    pos = jnp.arange(t)
    lb_all = jnp.cumsum(jax.nn.softmax(rec_lower_bound_logits.astype(jnp.float32), axis=0), axis=0)
    h = x
    for layer in range(DEPTH):
        u = rms_norm(h, norm_w[layer])
        proj = jnp.einsum('btd,de->bte', u, w_in[layer])
        a_q, a_k, a_v, a_g, r_q, r_f, r_i, r_g = jnp.split(proj, SPLIT_POINTS, axis=-1)

        q = partial_rope(to_heads(a_q, ATTN_HEADS), pos)
        k = partial_rope(to_heads(a_k, ATTN_HEADS), pos)
        attn = from_heads(moba_attention(q, k, to_heads(a_v, ATTN_HEADS)))
        attn = attn * jax.nn.silu(a_g)

        lb = lb_all[layer]
        z = r_f.astype(jnp.float32)
        log_f = jnp.log(lb + (1.0 - lb) * jax.nn.sigmoid(z))
        k_r = (1.0 - lb) * jax.nn.sigmoid(-z)
        q_r = jax.nn.silu(r_q.astype(jnp.float32)) * REC_KEY_DIM ** -0.5
        v_r = r_i.astype(jnp.float32)
        o_r = hgrn2_recurrence(to_heads(q_r, REC_HEADS), to_heads(k_r, REC_HEADS),
                               to_heads(v_r, REC_HEADS), to_heads(log_f, REC_HEADS))
        o_r = o_r * lax.rsqrt(jnp.mean(o_r * o_r, axis=-1, keepdims=True) + NORM_EPS)
        rec = (from_heads(o_r) * rec_out_norm_w[layer].astype(jnp.float32)).astype(h.dtype)
        rec = rec * jax.nn.silu(r_g)

        mixed = jnp.concatenate([attn, rec], axis=-1)
        h = h + jnp.einsum('bte,ed->btd', mixed, w_out[layer])
    return rms_norm(h, final_norm_w)
```

```python
import bisect
from contextlib import ExitStack

import numpy as np
import ml_dtypes
import concourse.bass as bass
import concourse.mybir as mybir
from concourse.bass_utils import run_bass_kernel_spmd

F32 = mybir.dt.float32
BF16 = mybir.dt.bfloat16
AF = mybir.ActivationFunctionType
ALU = mybir.AluOpType
AX = mybir.AxisListType

T = 2048
D = 4096
NCH = 32
NH = 16
EPS = 1e-6
SC = 128.0 ** -0.5
NEG = -30000.0

CF_ID, CF_TRI, CF_SCAN, CF_ROPE, NCF = 0, 128, 192, 704, 2752
CB_ID, CB_CAUS, CB_ESEL, NCB = 0, 128, 256, 1280


class Buf:
    __slots__ = ("name", "w", "r")

    def __init__(self, name):
        self.name = name
        self.w = None
        self.r = {}


class Prog:
    ENGS = ("pe", "act", "dve", "pool", "sp")

    def __init__(self, nc):
        self.nc = nc
        self.streams = {e: [] for e in self.ENGS}
        self.sigpos = {e: [] for e in self.ENGS}
        self.lastc = {e: -1 for e in self.ENGS}
        self.waited = {e: {} for e in self.ENGS}
        self.dmacnt = {}

    def _resolve(self, t):
        if t[0] == "dma":
            return (t[1], t[2])
        _, e, idx = t
        sp = self.sigpos[e]
        i = bisect.bisect_left(sp, idx)
        if i < len(sp):
            return (e, i + 1)
        pos = self.lastc[e]
        assert pos >= idx
        self.streams[e][pos]["sig"] = True
        sp.append(pos)
        return (e, len(sp))

    def _waits(self, eng, reads, writes, extra=()):
        ws = {}

        def need(t, war=False):
            if t is None:
                return
            if t[0] == "eng" and t[1] == eng and (eng == "pe" or war):
                return
            k, v = self._resolve(t)
            if self.waited[eng].get(k, 0) >= v:
                return
            if ws.get(k, 0) < v:
                ws[k] = v

        for b in reads:
            need(b.w)
        for b in writes:
            need(b.w)
            for t in b.r.values():
                need(t, war=True)
        for t in extra:
            need(t)
        for k, v in ws.items():
            self.waited[eng][k] = v
        return list(ws.items())

    @staticmethod
    def _addr(b, t):
        key = (t[0], t[1])
        old = b.r.get(key)
        if old is None or old[2] < t[2]:
            b.r[key] = t

    def op(self, eng, fn, reads=(), writes=(), sig=None):
        px = [b for b in reads if b.name.startswith("psum")]
        if px:
            reads = [b for b in reads if not b.name.startswith("psum")]
            writes = list(writes) + [b for b in px if b not in writes]
        waits = self._waits(eng, reads, writes)
        st = self.streams[eng]
        pos = len(st)
        if sig is None:
            sig = getattr(fn, "sig", eng != "pe")
        st.append({"fn": fn, "waits": waits, "sig": bool(sig), "dma": None})
        if sig:
            self.sigpos[eng].append(pos)
        self.lastc[eng] = pos
        t = ("eng", eng, pos)
        for b in reads:
            self._addr(b, t)
        for b in writes:
            b.w = t
            b.r = {}
        return t

    def dma(self, eng, fn, sem, reads=(), writes=(), extra=()):
        waits = self._waits(eng, reads, writes, extra=extra)
        self.dmacnt[sem] = self.dmacnt.get(sem, 0) + 16
        self.streams[eng].append({"fn": fn, "waits": waits, "sig": False, "dma": sem})
        t = ("dma", sem, self.dmacnt[sem])
        for b in reads:
            self._addr(b, t)
        for b in writes:
            b.w = t
            b.r = {}
        return t

    def barrier(self, engs=None):
        tickets = [("eng", e, self.lastc[e]) for e in self.ENGS if self.lastc[e] >= 0]
        tickets += [("dma", s, c) for s, c in self.dmacnt.items()]
        for e in (engs or self.ENGS):
            waits = self._waits(e, (), (), extra=tickets)
            self.streams[e].append({"fn": None, "waits": waits, "sig": False, "dma": None})

    def final_wait(self, eng, sems):
        waits = [(s, self.dmacnt[s]) for s in sems if s in self.dmacnt]
        self.streams[eng].append({"fn": None, "waits": waits, "sig": False, "dma": None})

    def emit(self):
        nc = self.nc
        with ExitStack() as es:
            sems = {}
            for k in list(self.ENGS) + sorted(self.dmacnt.keys()):
                sems[k] = es.enter_context(nc.semaphore("s_" + k))
            block = es.enter_context(nc.Block())

            def run(engname):
                def body(eng):
                    for it in self.streams[engname]:
                        for k, v in it["waits"]:
                            eng.wait_ge(sems[k], v)
                        if it["fn"] is None:
                            continue
                        ins = it["fn"](eng)
                        if it["dma"] is not None:
                            ins.then_inc(sems[it["dma"]], 16)
                        elif it["sig"]:
                            ins.then_inc(sems[engname], 1)
                return body

            block.tensor(run("pe"))
            block.scalar(run("act"))
            block.vector(run("dve"))
            block.gpsimd(run("pool"))
            block.sync(run("sp"))


def build_program(n_att=NH, n_rec=NH, phase_b=True, taps=()):
    nc = bass.Bass("TRN2", target_bir_lowering=False)

    def din(name, shape, dt=F32):
        return nc.dram_tensor(name, list(shape), dt, kind="ExternalInput").ap()

    x = din("x", [T, D])
    w_in = din("w_in", [128, 128, D])
    w_out = din("w_out", [8, 128, NCH * 512])
    nw = din("nw", [1, D])
    fw = din("fw", [1, D])
    lgt = din("lgt", [128, 32])
    rwt = din("rwt", [128, 16])
    cf = din("cf", [128, NCF])
    cb = din("cb", [128, NCB], BF16)
    out = nc.dram_tensor("out", [T, D], F32, kind="ExternalOutput").ap()
    wob = nc.dram_tensor("wob", [8, 128, NCH * 512], BF16, kind="Internal").ap()
    mixd = nc.dram_tensor("mixd", [16, 128, NCH * 128], BF16, kind="Internal").ap()
    tap_out = {}
    for name, shape, dt in taps:
        tap_out[name] = nc.dram_tensor("tap_" + name, list(shape), dt, kind="ExternalOutput").ap()

    es = ExitStack()
    with es:
        AR = 105000
        arena = es.enter_context(nc.sbuf_tensor("arena", [128, AR], BF16))
        ps = es.enter_context(nc.psum_tensor("ps", [128, 4096], F32))
        psb = ps[:, :].bitcast(BF16)

        def vb(off, n):
            assert off % 2 == 0 and off // 2 + n <= AR
            return arena[:, off // 2: off // 2 + n]

        def vf(off, n):
            assert off % 4 == 0 and off // 2 + 2 * n <= AR
            return arena[:, off // 2: off // 2 + 2 * n].bitcast(F32)

        def bank(i):
            return ps[:, i * 512:(i + 1) * 512]

        def bankb(i):
            return psb[:, i * 1024:(i + 1) * 1024]

        P = Prog(nc)
        PB = [Buf("psum%d" % i) for i in range(8)]

        XT = 0
        WS = 131072
        HR = WS + 3 * 8192
        CR = HR + 32768
        xT = vb(XT, NCH * T)
        xT3 = xT.rearrange("p (c t) -> p c t", c=NCH)
        wslot = [vb(WS + i * 8192, 4096) for i in range(3)]
        Bw = [Buf("w%d" % i) for i in range(3)]
        BxT = [Buf("xT%d" % i) for i in range(16)]

        o = CR
        cfs = vf(o, NCF); o += NCF * 4
        cbs = vb(o, NCB); o += NCB * 2
        ones_b = vb(o, 128); o += 256
        ones_f = vf(o, 128); o += 512
        lgs = vf(o, 32); o += 128
        lb = vf(o, 16); o += 64
        oml = vf(o, 16); o += 64
        noml = vf(o, 16); o += 64
        rws = vf(o, 16); o += 64
        lncb = vf(o, 1); o += 4
        epsb = vf(o, 1); o += 4
        oneb = vf(o, 1); o += 4
        ss = vf(o, 1); o += 4
        lnv = vf(o, 1); o += 4
        rstd1 = vf(o, 1); o += 4
        o += 8
        km_b = vb(o, 8); o += 16
        ksum = vf(o, 8); o += 32
        gp = vf(o, 8); o += 32
        m8 = vf(o, 8); o += 32
        sel = vf(o, 8); o += 32
        negm = vf(o, 8); o += 32
        gsm = [(gp, m8, sel, negm), tuple(vf(o + 32 * i_, 8) for i_ in range(4))]; o += 128
        dcs = vf(o, 32); o += 128
        Tst = vf(o, 128); o += 512
        Sbf = vb(o, 128); o += 256
        Sbf_b = vb(o, 128); o += 256
        scb = vb(o, 64); o += 128
        gset = [tuple(vf(o + 96 * i_ + 32 * k_, 8) for k_ in range(3)) for i_ in range(8)]; o += 768
        negmB = [vb(o + 256 * i_, 128) for i_ in range(8)]; o += 2048
        scbz = [vb(o, 64), vb(o + 128, 64)]; o += 256
        assert o <= AR * 2, o
        Bc = Buf("consts")
        ident_f = cfs[:, CF_ID:CF_ID + 128]
        tri01 = cfs[:, CF_TRI:CF_TRI + 64]
        scanm = cfs[:, CF_SCAN:CF_SCAN + 512]
        rope = cfs[:, CF_ROPE:CF_ROPE + T]
        ident_b = cbs[:, CB_ID:CB_ID + 128]
        causneg = cbs[:, CB_CAUS:CB_CAUS + 128]
        esel = cbs[:, CB_ESEL:CB_ESEL + 1024]

        class _F:
            def __init__(self, f, sig):
                self.f = f
                self.sig = sig

            def __call__(self, e):
                return self.f(e)

        def mm(out_, lhsT, rhs, start, stop):
            return _F(lambda e: e.matmul(out_, lhsT, rhs, start=start, stop=stop), bool(stop))

        def tr(out_, in_, idt):
            return _F(lambda e: e.transpose(out_, in_, idt), True)

        def act(out_, in_, func, scale=None, bias=None, accum=None):
            kw = {}
            if scale is not None:
                kw["scale"] = scale
            if bias is not None:
                kw["bias"] = bias
            if accum is not None:
                kw["accum_out"] = accum
            return lambda e: e.activation(out_, in_, func, **kw)

        def tt(out_, a, b, op):
            return lambda e: e.tensor_tensor(out_, a, b, op)

        def ts(out_, a, s1, s2, op0, op1=None):
            if op1 is None:
                return lambda e: e.tensor_scalar(out_, a, s1, None, op0)
            return lambda e: e.tensor_scalar(out_, a, s1, s2, op0, op1)

        def stt(out_, a, s, b, op0, op1):
            return lambda e: e.scalar_tensor_tensor(out_, a, s, b, op0, op1)

        def cp(out_, in_):
            return lambda e: e.tensor_copy(out_, in_)

        def mset(out_, v):
            return lambda e: e.memset(out_, v)

        def dmaf(out_, in_):
            return lambda e: e.dma_start(out=out_, in_=in_)

        def tap(name, ap, bufs):
            if name in tap_out:
                P.dma("sp", dmaf(tap_out[name], ap), "tap", reads=bufs)

        P.dma("sp", dmaf(cfs, cf), "ldc0", writes=[Bc])
        P.dma("sp", dmaf(cbs, cb), "ldc1", writes=[Bc])
        P.dma("sp", dmaf(lgs, lgt), "ldc2", writes=[Bc])
        P.dma("sp", dmaf(rws, rwt), "ldc3", writes=[Bc])
        P.op("pool", mset(ones_b, 1.0), writes=[Bc])
        P.op("pool", mset(ones_f, 1.0), writes=[Bc])
        P.op("pool", mset(lncb, float(np.log(SC))), writes=[Bc])
        P.op("pool", mset(epsb, EPS), writes=[Bc])
        P.op("pool", mset(oneb, 1.0), writes=[Bc])
        Bz = Buf("zeropad")
        for i_ in range(8):
            P.op("pool", mset(negmB[i_], 0.0), writes=[Bz])
        Bg8 = [Buf("g8_%d" % i_) for i_ in range(8)]
        P.op("pool", mset(scbz[0], 0.0), writes=[Bz])
        P.op("pool", mset(scbz[1], 0.0), writes=[Bz])
        P.op("dve", tt(lb, lgs[:, 16:32], lgs[:, 0:16], ALU.subtract), reads=[Bc], writes=[Bc])
        P.op("act", act(lb, lb, AF.Exp), reads=[Bc], writes=[Bc])
        P.op("dve", ts(lb, lb, 1.0, None, ALU.add), reads=[Bc], writes=[Bc])
        P.op("dve", lambda e: e.reciprocal(lb, lb), reads=[Bc], writes=[Bc])
        P.op("dve", ts(oml, lb, -1.0, 1.0, ALU.mult, ALU.add), reads=[Bc], writes=[Bc])
        P.op("dve", ts(noml, oml, -1.0, None, ALU.mult), reads=[Bc], writes=[Bc])

        xs2 = [vf(WS, D), vf(WS + 16384, D)]
        u_b = vb(WS + 32768, D)
        nwb = vf(WS + 40960, D)
        Bxs2, Bu, Bnw, Bst = [Buf("xs0"), Buf("xs1")], Buf("u"), Buf("nwb"), Buf("stat")
        P.dma("sp", dmaf(nwb, nw[0:1, :].partition_broadcast(128)), "ldc4", writes=[Bnw])
        ev = 0
        for tt_ in range(16):
            xs, Bxs = xs2[tt_ % 2], Bxs2[tt_ % 2]
            P.dma("sp", dmaf(xs, x[tt_ * 128:(tt_ + 1) * 128, :]), "ldx%d" % (tt_ % 2), writes=[Bxs])
            P.op("act", act(u_b, xs, AF.Square, accum=ss), reads=[Bxs], writes=[Bu, Bst])
            P.op("act", act(lnv, ss, AF.Ln, scale=1.0 / D, bias=epsb), reads=[Bst, Bc], writes=[Bst])
            P.op("act", act(rstd1, lnv, AF.Exp, scale=-0.5), reads=[Bst], writes=[Bst])
            P.op("dve", stt(u_b, xs, rstd1, nwb, ALU.mult, ALU.mult), reads=[Bxs, Bst, Bnw], writes=[Bu])
            for c0 in range(0, NCH, 8):
                bk = 4 + (ev % 4)
                for c in range(c0, c0 + 8):
                    P.op("pe", tr(bankb(bk)[:, (c - c0) * 128:(c - c0 + 1) * 128],
                                  u_b[:, c * 128:(c + 1) * 128], ident_b),
                         reads=[Bu, Bc], writes=[PB[bk]])
                dst = xT3[:, c0:c0 + 8, tt_ * 128:(tt_ + 1) * 128]
                src = bankb(bk).rearrange("p (c t) -> p c t", c=8)
                if ev % 2 == 0:
                    P.op("act", lambda e, d=dst, s=src: e.copy(d, s), reads=[PB[bk]], writes=[BxT[tt_]])
                else:
                    P.op("dve", cp(dst, src), reads=[PB[bk]], writes=[BxT[tt_]])
                ev += 1
        tap("xT", xT3[:, 0, :], BxT)
        P.barrier()

        sched = []
        for h in range(n_att):
            sched += [h, 16 + h, 32 + h, 48 + h]
        for h in range(n_rec):
            sched += [64 + h, 80 + h, 96 + h, 112 + h]
        NWS = 2
        wstate = {"next": 0, "slot_of": {}}

        def prefetch(upto):
            while wstate["next"] < min(upto, len(sched)):
                i = wstate["next"]
                s = i % NWS
                P.dma("pool", dmaf(wslot[s], w_in[sched[i]]), "ldw%d" % s, writes=[Bw[s]])
                wstate["slot_of"][i] = s
                wstate["next"] += 1

        widx = {"i": 0, "bank": 0}

        def inproj_gen(evac, unit=8, state=None, banks=(0, 1)):
            i = widx["i"]
            widx["i"] += 1
            prefetch(i + NWS)
            s = wstate["slot_of"][i]
            for g in range(4):
                bk = banks[widx["bank"] % len(banks)]
                widx["bank"] += 1
                for c in range(NCH):
                    rd = ([Bw[s]] + BxT[4 * g:4 * g + 4]) if c == 0 else []
                    P.op("pe", mm(bank(bk), wslot[s][:, c * 128:(c + 1) * 128],
                                  xT3[:, c, g * 512:(g + 1) * 512], c == 0, c == NCH - 1),
                         reads=rd, writes=[PB[bk]])
                    if c == NCH - 1:
                        tl = ("eng", "pe", len(P.streams["pe"]) - 1)
                        for b_ in BxT[4 * g:4 * g + 4]:
                            Prog._addr(b_, tl)
                        if g == 3:
                            Prog._addr(Bw[s], tl)
                    if (c + 1) % unit == 0 and c != NCH - 1:
                        yield
                evac(g, bk)
                if state is not None:
                    state["done"] = g + 1
                yield

        def run(gen):
            for _ in gen:
                pass

        def step(gen, n=1):
            for _ in range(n):
                try:
                    next(gen)
                except StopIteration:
                    return False
            return True

        AUX = WS + 2 * 8192
        qT = vb(HR + 0, T)
        kT = vb(HR + 4096, T)
        vTm = vb(HR + 8192, T)
        v_tm = vb(HR + 12288, T)
        sg = vb(HR + 16384, T)
        PT = [vb(HR + 20480 + i * 1024, 512) for i in range(2)] + [vb(HR + 22528, 512)]
        tA = vf(HR + 22528, 512)
        tB = vf(HR + 24576, 512)
        rden = vf(HR + 26624, 512)
        tmpo = vf(HR + 28672, 512)
        negmT = vb(AUX, T)
        qs = vf(AUX, T)
        k_tm = vb(HR + 20480, T)
        tmp1 = vf(HR + 24576, 512)
        tmp2 = vf(HR + 26624, 512)
        tmp3 = vf(HR + 28672, 512)
        tmp4 = vf(HR + 30720, 512)
        BqT, BkT, BvT, Bvtm, Bsg = Buf("qT"), Buf("kT"), Buf("vT"), Buf("vtm"), Buf("sg")
        Bmix = Buf("mix")
        BPT = [Buf("PT0"), Buf("PT1"), None]
        BtA, BtB = Buf("tA"), Buf("tB")
        BPT[2] = BtA
        Bng = Buf("negmT")
        Brd, Bto, Bsm = Buf("rden"), Buf("tmpo"), Buf("small")
        Bgs = [Buf("gs0"), Buf("gs1")]

        def rope_evac(dstT, Bdst):
            def f(g, bk):
                tc_ = slice(g * 512, (g + 1) * 512)
                Pg = bank(bk)
                P.op("act", lambda e, d=dstT[:, tc_], s=Pg: e.copy(d, s), reads=[PB[bk]], writes=[Bdst])
                P.op("dve", tt(tA[0:16, :], Pg[0:16, :], rope[0:16, tc_], ALU.mult), reads=[PB[bk], Bc], writes=[BtA])
                P.op("dve", tt(tB[0:16, :], Pg[64:80, :], rope[32:48, tc_], ALU.mult), reads=[PB[bk], Bc], writes=[BtB])
                P.op("dve", tt(tA[64:80, :], Pg[64:80, :], rope[64:80, tc_], ALU.mult), reads=[PB[bk], Bc], writes=[BtA])
                P.op("dve", tt(tB[64:80, :], Pg[0:16, :], rope[96:112, tc_], ALU.mult), reads=[PB[bk], Bc], writes=[BtB])
                P.op("pool", tt(dstT[0:16, tc_], tA[0:16, :], tB[0:16, :], ALU.subtract), reads=[BtA, BtB], writes=[Bdst])
                P.op("pool", tt(dstT[64:80, tc_], tA[64:80, :], tB[64:80, :], ALU.add), reads=[BtA, BtB], writes=[Bdst])
            return f

        def copy_evac(dstT, Bdst):
            def f(g, bk):
                d = dstT[:, g * 512:(g + 1) * 512]
                if g % 2 == 0:
                    P.op("act", lambda e, d=d, s=bank(bk): e.copy(d, s), reads=[PB[bk]], writes=[Bdst])
                else:
                    P.op("dve", cp(d, bank(bk)), reads=[PB[bk]], writes=[Bdst])
            return f

        def silu_evac(dstT, Bdst, tmpE, BtE):
            def f(g, bk):
                d = dstT[:, g * 512:(g + 1) * 512]
                P.op("act", act(tmpE, bank(bk), AF.Exp, scale=-1.0), reads=[PB[bk]], writes=[BtE])
                P.op("act", act(tmpE, tmpE, AF.Ln, bias=oneb), reads=[BtE, Bc], writes=[BtE])
                P.op("act", act(tmpE, tmpE, AF.Exp, scale=-1.0), reads=[BtE], writes=[BtE])
                P.op("dve", tt(d, bank(bk), tmpE, ALU.mult), reads=[PB[bk], BtE], writes=[Bdst])
            return f

        def to_token_major(srcT, Bsrc, dst_tm, Bdst):
            for half in range(2):
                bk = 2 + half
                for j in range(8):
                    t_ = half * 8 + j
                    P.op("pe", tr(bankb(bk)[:, j * 128:(j + 1) * 128], srcT[:, t_ * 128:(t_ + 1) * 128], ident_b),
                         reads=[Bsrc, Bc], writes=[PB[bk]])
                d = dst_tm[:, half * 1024:(half + 1) * 1024]
                if half == 0:
                    P.op("act", lambda e, d=d, s=bankb(bk): e.copy(d, s), reads=[PB[bk]], writes=[Bdst])
                else:
                    P.op("dve", cp(d, bankb(bk)), reads=[PB[bk]], writes=[Bdst])

        def gate_chain():
            for t_ in range(8, 16):
                P.op("pe", mm(bank(2)[:, (t_ - 8) * 8:(t_ - 8) * 8 + 8], qT[:, t_ * 128:(t_ + 1) * 128], km_b, True, True),
                     reads=[BqT, Bsm], writes=[PB[2]])
            for t_ in range(8, 16):
                i_ = t_ - 8
                b = t_ // 2
                gp_, m8_, sel_ = gset[i_]
                nb_ = negmB[i_]
                gsrc = bank(2)[:, i_ * 8:i_ * 8 + b]
                P.op("pool", mset(gp_, -1e30), reads=[Bz], writes=[Bg8[i_]])
                P.op("pool", mset(nb_[:, 0:8], 0.0), reads=[Bz], writes=[Bg8[i_]])
                P.op("dve", cp(gp_[:, 0:b], gsrc), reads=[PB[2]], writes=[Bg8[i_]])
                P.op("dve", lambda e, m=m8_, g_=gp_: e.max(m, g_), reads=[Bg8[i_]], writes=[Bg8[i_]])
                P.op("dve", ts(sel_[:, 0:b], gp_[:, 0:b], m8_[:, 2:3], None, ALU.is_ge), reads=[Bg8[i_]], writes=[Bg8[i_]])
                P.op("dve", ts(nb_[:, 0:b], sel_[:, 0:b], -1.0, -NEG, ALU.add, ALU.mult), reads=[Bg8[i_]], writes=[Bg8[i_]])

        def gate_finish():
            for i_ in range(8):
                P.op("pe", tr(bankb(3)[:, i_ * 128:(i_ + 1) * 128], negmB[i_], ident_b),
                     reads=[Bg8[i_], Bc], writes=[PB[3]])
            P.op("act", lambda e: e.copy(negmT[:, 1024:2048], bankb(3)), reads=[PB[3]], writes=[Bng])

        def att_main_gen(h):
            for g in range(4):
                kts = list(range(4 * g + 4))

                bO, bD = (6, 7) if g % 2 == 0 else (2, 3)

                def pv(kt, slot, col0, first, last, bO=bO, bD=bD):
                    w = 512 - col0
                    P.op("pe", mm(bank(bO)[:, col0:512], v_tm[:, kt * 128:(kt + 1) * 128], PT[slot][:, 0:w], first, last),
                         reads=[Bvtm, BPT[slot]], writes=[PB[bO]])
                    P.op("pe", mm(bank(bD)[:, col0:512], ones_b, PT[slot][:, 0:w], first, last),
                         reads=[Bc, BPT[slot]], writes=[PB[bD]])

                pend = []
                for i, kt in enumerate(kts):
                    slot = i % 2
                    bk = 4 + slot
                    col0 = max(0, kt - 4 * g) * 128
                    w = 512 - col0
                    diag = kt >= 4 * g
                    j = kt // 2
                    need_mask = (g >= 2) and (j < 2 * g + 1)
                    P.op("pe", mm(bank(bk)[:, 0:w], kT[:, kt * 128:(kt + 1) * 128],
                                  qT[:, g * 512 + col0:(g + 1) * 512], True, not (diag or need_mask)),
                         reads=[BkT, BqT], writes=[PB[bk]])
                    if diag:
                        P.op("pe", mm(bank(bk)[:, 0:128], ident_b, causneg, False, not need_mask),
                             reads=[Bc], writes=[PB[bk]])
                    if need_mask:
                        P.op("pe", mm(bank(bk)[:, 0:w], esel[:, j * 128:(j + 1) * 128],
                                      negmT[:, g * 512 + col0:(g + 1) * 512], False, True),
                             reads=[Bc, Bng], writes=[PB[bk]])
                    P.op("act", act(PT[slot][:, 0:w], bank(bk)[:, 0:w], AF.Exp, scale=SC),
                         reads=[PB[bk]], writes=[BPT[slot]])
                    pend.append((kt, slot, col0, i == 0, i == len(kts) - 1))
                    if len(pend) > 1:
                        pv(*pend.pop(0))
                    yield ("step", g)
                while pend:
                    pv(*pend.pop(0))
                yield ("need", g)
                gc = slice(g * 512, (g + 1) * 512)
                P.op("act", act(rden, bank(bD), AF.Ln), reads=[PB[bD]], writes=[Brd])
                P.op("act", act(rden, rden, AF.Exp, scale=-1.0), reads=[Brd], writes=[Brd])
                P.op("dve", tt(tmpo, bank(bO), rden, ALU.mult), reads=[PB[bO], Brd], writes=[Bto])
                P.op("pool", tt(vTm[:, gc], tmpo, sg[:, gc], ALU.mult), reads=[Bto, Bsg], writes=[Bmix])
            P.dma("sp", dmaf(mixd[:, :, h * 128:(h + 1) * 128].rearrange("a p t -> p a t"), vTm.rearrange("p (a t) -> p a t", a=16)), "stm", reads=[Bmix])

        def drive(main, ip, ipstate, per_step):
            alive = True
            for item in main:
                kind, g = item
                if kind == "need":
                    while alive and ipstate["done"] <= g:
                        alive = step(ip)
                elif alive:
                    alive = step(ip, per_step[g])
            while alive:
                alive = step(ip)

        Bwob = [Buf("wob%d" % i) for i in range(8)]
        for h in range(n_att):
            if h < 8:
                P.dma("pool", dmaf(wob[h], w_out[h]), "cvt%d" % h, writes=[Bwob[h]])
            run(inproj_gen(rope_evac(qT, BqT), banks=(0, 1, 2, 3)))
            run(inproj_gen(rope_evac(kT, BkT), banks=(0, 1, 2, 3)))
            P.op("dve", lambda e: e.tensor_reduce(ksum, kT.rearrange("p (b t) -> p b t", b=8), AX.X, ALU.add),
                 reads=[BkT], writes=[Bsm])
            P.op("dve", ts(km_b, ksum, 1.0 / 256.0, None, ALU.mult), reads=[Bsm], writes=[Bsm])
            ipv = inproj_gen(copy_evac(vTm, Bmix), unit=8)
            step(ipv, 10)
            gate_chain()
            run(ipv)
            to_token_major(vTm, Bmix, v_tm, Bvtm)
            gate_finish()
            st = {"done": 0}
            ipg = inproj_gen(silu_evac(sg, Bsg, tB, BtB), unit=1, state=st, banks=(0, 1))
            drive(att_main_gen(h), ipg, st, {0: 8, 1: 4, 2: 3, 3: 2})
            if h == 0:
                tap("qT", qT, [BqT]); tap("kT", kT, [BkT]); tap("vtm", v_tm, [Bvtm]); tap("sg", sg, [Bsg])
                tap("mix0", vTm, [Bmix])

        P.barrier()

        BqpT, BkpT, Bktm, Bqs = Buf("qpT"), Buf("kpT"), Buf("ktm"), Buf("qs")
        Bt1, Bt2, Bt3, Bt4 = Buf("tmp1"), Buf("tmp2"), Buf("tmp3"), Buf("tmp4")
        Bdc, BT = Buf("dc"), Buf("T")
        BS = [Buf("Sbf0"), Buf("Sbf1")]
        Bsc = [Buf("scb0"), Buf("scb1")]
        qpT, kpT = qT, kT

        e2buf = vf(HR + 20480, 512)
        ebuf = [tmp1, e2buf]
        k_tmH = [k_tm, vb(AUX, T)]
        Be = [Bt1, Bktm]

        def rf_evac(h):
            hc = slice(h, h + 1)

            def head(g, bk):
                P.op("act", act(ebuf[g % 2], bank(bk), AF.Exp), reads=[PB[bk]], writes=[Be[g % 2]])

            def tail(g):
                gc = slice(g * 512, (g + 1) * 512)
                eb, Beb = ebuf[g % 2], Be[g % 2]
                P.op("act", act(tmp2, eb, AF.Ln, bias=lb[:, hc]), reads=[Beb, Bc], writes=[Bt2])
                P.op("act", act(tmp4, eb, AF.Ln, bias=oneb), reads=[Beb, Bc], writes=[Bt4])
                P.op("dve", tt(tmp2, tmp2, tmp4, ALU.subtract), reads=[Bt2, Bt4], writes=[Bt2])
                P.op("dve", lambda e: e.tensor_tensor_scan(tmp3, scanm, tmp2, 0.0, ALU.mult, ALU.add),
                     reads=[Bt2, Bc], writes=[Bt3])
                P.op("act", act(eb, tmp4, AF.Exp, scale=-1.0), reads=[Bt4], writes=[Beb])
                P.op("act", act(tmp2, tmp3, AF.Exp, scale=-1.0), reads=[Bt3], writes=[Bt2])
                P.op("dve", stt(kpT[:, gc], eb, oml[:, hc], tmp2, ALU.mult, ALU.mult),
                     reads=[Beb, Bt2, Bc], writes=[BkpT])
                P.op("act", act(dcs[:, g * 8:(g + 1) * 8], tmp3.rearrange("p (c j) -> p c j", j=64)[:, :, 63], AF.Exp),
                     reads=[Bt3], writes=[Bdc])
                P.op("act", act(tmp4, tmp3, AF.Exp, bias=lncb), reads=[Bt3, Bc], writes=[Bt4])
                P.op("dve", tt(qpT[:, gc], qs[:, gc], tmp4, ALU.mult), reads=[Bqs, Bt4], writes=[BqpT])

            def f(g, bk):
                head(g, bk)
                if g > 0:
                    tail(g - 1)
                if g == 3:
                    tail(3)
            return f

        def rec_main_gen(h):
            hc = slice(h, h + 1)

            def front(n):
                t_ = n // 2
                cs = slice(n * 64, (n + 1) * 64)
                tcs = slice(t_ * 128, (t_ + 1) * 128)
                bk = 4 if n % 2 == 0 else 6
                P.op("pe", mm(bank(bk)[:, 0:64], kpT[:, tcs], qpT[:, cs], True, True),
                     reads=[BkpT, BqpT], writes=[PB[bk]])
                P.op("pe", mm(bank(bk)[:, 128:256], k_tmH[n % 2][:, tcs], v_tm[:, tcs], True, True),
                     reads=[Bktm, Bqs, Bvtm], writes=[PB[bk]])

            def fin_a(g):
                bo = 2 + (g % 2)
                P.op("act", act(tmp1, bank(bo), AF.Square), reads=[PB[bo]], writes=[Bt1])

            def fin_a2(g):
                bn = 5 if g % 2 == 0 else 7
                P.op("pe", mm(bank(bn), ones_f, tmp1, True, True), reads=[Bc, Bt1], writes=[PB[bn]])

            def fin_b(g):
                bn = 5 if g % 2 == 0 else 7
                P.op("act", act(tmp2, bank(bn), AF.Ln, scale=1.0 / 128.0, bias=epsb), reads=[PB[bn], Bc], writes=[Bt2])
                P.op("act", act(tmp2, tmp2, AF.Exp, scale=-0.5), reads=[Bt2], writes=[Bt2])

            def fin_c(g):
                bo = 2 + (g % 2)
                gc = slice(g * 512, (g + 1) * 512)
                P.op("dve", stt(tmp3, bank(bo), rws[:, hc], tmp2, ALU.mult, ALU.mult),
                     reads=[PB[bo], Bc, Bt2], writes=[Bt3])
                P.op("pool", tt(vTm[:, gc], tmp3, sg[:, gc], ALU.mult), reads=[Bt3, Bsg], writes=[Bmix])

            front(0)
            for n in range(32):
                t_ = n // 2
                r0 = (n % 2) * 64
                par = n % 2
                cs = slice(n * 64, (n + 1) * 64)
                g = n // 8
                oc = slice((n % 8) * 64, (n % 8 + 1) * 64)
                bk = 4 if par == 0 else 6
                bo = 2 + (g % 2)
                P.op("dve", tt(scbz[par][r0:r0 + 64, :], bank(bk)[r0:r0 + 64, 0:64], tri01[r0:r0 + 64, :], ALU.mult),
                     reads=[PB[bk], Bc, Bz], writes=[Bsc[par]])
                if n == 0:
                    P.op("dve", cp(Tst, bank(bk)[:, 128:256]), reads=[PB[bk]], writes=[BT])
                else:
                    P.op("dve", stt(Tst, Tst, dcs[:, n - 1:n], bank(bk)[:, 128:256], ALU.mult, ALU.add),
                         reads=[BT, Bdc, PB[bk]], writes=[BT])
                if n < 31:
                    P.op("dve", ts(Sb2[par], Tst, dcs[:, n:n + 1], None, ALU.mult), reads=[BT, Bdc], writes=[BS[par]])
                    front(n + 1)
                P.op("pe", mm(bank(bo)[:, oc], v_tm[:, t_ * 128:(t_ + 1) * 128], scbz[par], True, n == 0),
                     reads=[Bvtm, Bsc[par]], writes=[PB[bo]])
                if n > 0:
                    P.op("pe", mm(bank(bo)[:, oc], Sb2[1 - par], qpT[:, cs], False, True),
                         reads=[BS[1 - par], BqpT], writes=[PB[bo]])
                if g > 0 and n % 8 == 1:
                    fin_a2(g - 1)
                if g > 0 and n % 8 == 3:
                    fin_b(g - 1)
                yield ("step", g)
                if g > 0 and n % 8 == 4:
                    yield ("need", g - 1)
                    fin_c(g - 1)
                if n % 8 == 7:
                    fin_a(g)
            fin_a2(3)
            fin_b(3)
            yield ("need", 3)
            fin_c(3)
            P.dma("sp", dmaf(mixd[:, :, (16 + h) * 128:(17 + h) * 128].rearrange("a p t -> p a t"),
                             vTm.rearrange("p (a t) -> p a t", a=16)), "stm", reads=[Bmix])

        Sb2 = [Sbf, Sbf_b]
        for h in range(n_rec):
            run(inproj_gen(silu_evac(qs, Bqs, tmp4, Bt4), banks=(0, 1, 2, 3)))
            run(inproj_gen(rf_evac(h), banks=(0, 1, 2, 3)))
            run(inproj_gen(copy_evac(vTm, Bmix), banks=(0, 1, 2, 3)))
            to_token_major(vTm, Bmix, v_tm, Bvtm)
            P.op("pool", mset(k_tmH[0][64:128, :], 0.0), writes=[Bktm])
            P.op("pool", mset(k_tmH[1][0:64, :], 0.0), writes=[Bqs])
            for half in range(2):
                bk = 2 + half
                for j in range(8):
                    t_ = half * 8 + j
                    P.op("pe", tr(bankb(bk)[:, j * 128:(j + 1) * 128], kpT[:, t_ * 128:(t_ + 1) * 128], ident_b),
                         reads=[BkpT, Bc], writes=[PB[bk]])
                hs_ = slice(half * 1024, (half + 1) * 1024)
                P.op("act", lambda e, d=k_tmH[0][0:64, hs_], s_=bankb(bk)[0:64, :]: e.copy(d, s_),
                     reads=[PB[bk]], writes=[Bktm])
                P.op("dve", cp(k_tmH[1][64:128, hs_], bankb(bk)[64:128, :]), reads=[PB[bk]], writes=[Bqs])
            st = {"done": 0}
            ipg = inproj_gen(silu_evac(sg, Bsg, tmp4, Bt4), unit=1, state=st)
            drive(rec_main_gen(h), ipg, st, {0: 4, 1: 4, 2: 4, 3: 4})
            if h == 0:
                tap("qpT", qpT, [BqpT]); tap("kpT", kpT, [BkpT]); tap("dc", dcs, [Bdc])
                tap("mix16", vTm, [Bmix])

        if phase_b:
            mixq = vb(0, NCH * 512)
            wo = [vb(32768 + i * 32768, NCH * 512) for i in range(2)]
            xh = vf(98304, 4 * D)
            fwb = vf(163840, D)
            junk = vb(180224, D)
            o = 188416
            ssB = [vf(o + 16 * j, 1) for j in range(4)]
            lnB = [vf(o + 16 * j + 4, 1) for j in range(4)]
            rsB = [vf(o + 16 * j + 8, 1) for j in range(4)]
            epB = vf(o + 64, 1)
            Bmq = [Buf("mixq%d" % j) for j in range(4)]
            Bxs = [[Buf("xh%d_%d" % (j, n_)) for n_ in range(8)] for j in range(4)]
            Bfw, Bjk, Bep = Buf("fwb"), Buf("junk"), Buf("epB")
            BstB = [Buf("statB%d" % j) for j in range(4)]
            Bwo = [Buf("wo0"), Buf("wo1")]
            stm_t = [("dma", "stm", P.dmacnt["stm"])] if "stm" in P.dmacnt else []
            for j in range(4):
                P.dma("pool", dmaf(mixq[:, j * 4096:(j + 1) * 4096], mixd[j]), "ldm%d" % j,
                      writes=[Bmq[j]] + (BxT if j == 0 else []), extra=stm_t)
            wsched = [(q4, ng) for q4 in range(4) for ng in range(8)]

            def load_wo(i):
                s_ = i % 2
                ng_ = wsched[i][1]
                P.dma("act", dmaf(wo[s_], wob[ng_]), "ldo%d" % s_, reads=[Bwob[ng_]],
                      writes=[Bwo[s_]] + (BxT if i < 2 else []))

            load_wo(0)
            load_wo(1)
            P.barrier(engs=("act", "dve", "pool", "sp"))
            P.dma("sp", dmaf(fwb, fw[0:1, :].partition_broadcast(128)), "ldc5", writes=[Bfw])
            P.op("pool", mset(epB, EPS), writes=[Bep])
            def load_x(q4_):
                for n_ in range(8):
                    for j_ in range(4):
                        r_ = q4_ * 512 + j_ * 128
                        P.dma("sp", dmaf(xh[:, j_ * D + n_ * 512:j_ * D + (n_ + 1) * 512],
                                         x[r_:r_ + 128, n_ * 512:(n_ + 1) * 512]),
                              "lx%d_%d" % (j_, n_), writes=[Bxs[j_][n_]])

            load_x(0)
            for i, (q4, ng) in enumerate(wsched):
                s = i % 2
                wo3 = wo[s].rearrange("p (c n) -> p c n", c=NCH)
                for j in range(4):
                    bk = (ng * 4 + j) % 8
                    for c in range(NCH):
                        rd = [Bmq[j], Bwo[s]] if c == 0 else []
                        P.op("pe", mm(bank(bk), mixq[:, j * 4096 + c * 128:j * 4096 + (c + 1) * 128], wo3[:, c, :],
                                      c == 0, c == NCH - 1),
                             reads=rd, writes=[PB[bk]])
                    t = ("eng", "pe", len(P.streams["pe"]) - 1)
                    Prog._addr(Bwo[s], t)
                    Prog._addr(Bmq[j], t)
                    hs = xh[:, j * D + ng * 512: j * D + (ng + 1) * 512]
                    P.op("dve", tt(hs, bank(bk), hs, ALU.add), reads=[PB[bk]], writes=[Bxs[j][ng]])
                    if ng == 7:
                        if q4 < 3:
                            P.dma("pool", dmaf(mixq[:, j * 4096:(j + 1) * 4096], mixd[(q4 + 1) * 4 + j]), "ldm%d" % j,
                                  writes=[Bmq[j]])
                        r = q4 * 512 + j * 128
                        hj = xh[:, j * D:(j + 1) * D]
                        P.op("act", act(junk, hj, AF.Square, accum=ssB[j]), reads=Bxs[j], writes=[Bjk, BstB[j]])
                        P.op("act", act(lnB[j], ssB[j], AF.Ln, scale=1.0 / D, bias=epB), reads=[BstB[j], Bep], writes=[BstB[j]])
                        P.op("act", act(rsB[j], lnB[j], AF.Exp, scale=-0.5), reads=[BstB[j]], writes=[BstB[j]])
                        P.op("dve", stt(hj, hj, rsB[j], fwb, ALU.mult, ALU.mult), reads=[BstB[j], Bfw], writes=Bxs[j])
                        P.dma("sp", dmaf(out[r:r + 128, :], hj), "sto", reads=Bxs[j])
                if i + 2 < len(wsched):
                    load_wo(i + 2)
                if ng == 7 and q4 < 3:
                    load_x(q4 + 1)
        P.final_wait("sp", ["sto", "tap", "stm"])
        P.emit()
    return nc


def _constants():
    cf = np.zeros((128, NCF), np.float32)
    cf[:, CF_ID:CF_ID + 128] = np.eye(128, dtype=np.float32)
    s = np.arange(128)[:, None] % 64
    c = np.arange(64)[None, :]
    cf[:, CF_TRI:CF_TRI + 64] = (c >= s).astype(np.float32)
    sm = np.ones(512, np.float32)
    sm[::64] = 0.0
    cf[:, CF_SCAN:CF_SCAN + 512] = sm[None, :]
    half = 16
    inv_freq = np.power(np.float32(500000.0), -np.arange(half, dtype=np.float32) / np.float32(half)).astype(np.float32)
    ang = (np.arange(T, dtype=np.float32)[:, None] * inv_freq[None, :]).astype(np.float32)
    cos = np.cos(ang.astype(np.float64)).astype(np.float32).T
    sin = np.sin(ang.astype(np.float64)).astype(np.float32).T
    cf[0:16, CF_ROPE:CF_ROPE + T] = cos
    cf[32:48, CF_ROPE:CF_ROPE + T] = sin
    cf[64:80, CF_ROPE:CF_ROPE + T] = cos
    cf[96:112, CF_ROPE:CF_ROPE + T] = sin
    cb = np.zeros((128, NCB), np.float32)
    cb[:, CB_ID:CB_ID + 128] = np.eye(128)
    tk = np.arange(128)[:, None]
    tq = np.arange(128)[None, :]
    cb[:, CB_CAUS:CB_CAUS + 128] = np.where(tk > tq, NEG, 0.0)
    for j in range(8):
        cb[j, CB_ESEL + j * 128:CB_ESEL + (j + 1) * 128] = 1.0
    return cf, cb.astype(ml_dtypes.bfloat16)


_PERM = np.array([(2 * b1 + b0) * 16 + i for b0 in range(2) for b1 in range(4) for i in range(16)])


def _layout_inputs(inputs):
    w_in = np.asarray(inputs["w_in"])[0]
    w4 = w_in.reshape(NCH, 128, 128, 128)
    w_l = np.ascontiguousarray(w4.transpose(2, 1, 0, 3))
    w_l[0:32] = w_l[0:32][:, :, :, _PERM]
    w_l = w_l.reshape(128, 128, NCH * 128)
    w_out = np.asarray(inputs["w_out"])[0]
    wo = np.ascontiguousarray(w_out.reshape(NCH, 128, 8, 512).transpose(2, 1, 0, 3)).reshape(8, 128, NCH * 512)
    lg = np.asarray(inputs["rec_lower_bound_logits"])
    lgt = np.ascontiguousarray(lg.reshape(2, 16, 128).transpose(2, 0, 1)).reshape(128, 32)
    rw = np.asarray(inputs["rec_out_norm_w"])[0]
    rwt = np.ascontiguousarray(rw.reshape(16, 128).T)
    nw = np.asarray(inputs["norm_w"]).reshape(1, D)
    fw = np.asarray(inputs["final_norm_w"]).reshape(1, D)
    cf, cb = _constants()
    shared = {"w_in": w_l, "w_out": wo, "nw": nw, "fw": fw, "lgt": lgt, "rwt": rwt, "cf": cf, "cb": cb}
    return shared


def kernel(**inputs):
    x = np.asarray(inputs["x"])
    shared = _layout_inputs(inputs)
    nc = build_program()
    in_maps = []
    for b in range(8):
        m = dict(shared)
        m["x"] = np.ascontiguousarray(x[b])
        in_maps.append(m)
    res = run_bass_kernel_spmd(nc, in_maps, core_ids=list(range(8)))
    return np.stack([np.asarray(r["out"]) for r in res.results], axis=0).astype(np.float32)
```

```python
import bisect
from contextlib import ExitStack

import numpy as np
import ml_dtypes
import concourse.bass as bass
import concourse.mybir as mybir
from concourse.bass_utils import run_bass_kernel_spmd

F32 = mybir.dt.float32
BF16 = mybir.dt.bfloat16
AF = mybir.ActivationFunctionType
ALU = mybir.AluOpType
AX = mybir.AxisListType

T = 2048
D = 4096
NCH = 32
NH = 16
EPS = 1e-6
SC = 128.0 ** -0.5
NEG = -30000.0

CF_ID, CF_TRI, CF_SCAN, CF_ROPE, NCF = 0, 128, 192, 704, 2752
CB_ID, CB_CAUS, CB_ESEL, NCB = 0, 128, 256, 1280


class Buf:
    __slots__ = ("name", "w", "r")

    def __init__(self, name):
        self.name = name
        self.w = None
        self.r = {}


class Prog:
    ENGS = ("pe", "act", "dve", "pool", "sp")

    def __init__(self, nc):
        self.nc = nc
        self.streams = {e: [] for e in self.ENGS}
        self.sigpos = {e: [] for e in self.ENGS}
        self.lastc = {e: -1 for e in self.ENGS}
        self.waited = {e: {} for e in self.ENGS}
        self.dmacnt = {}

    def _resolve(self, t):
        if t[0] == "dma":
            return (t[1], t[2])
        _, e, idx = t
        sp = self.sigpos[e]
        i = bisect.bisect_left(sp, idx)
        if i < len(sp):
            return (e, i + 1)
        pos = self.lastc[e]
        assert pos >= idx
        self.streams[e][pos]["sig"] = True
        sp.append(pos)
        return (e, len(sp))

    def _waits(self, eng, reads, writes, extra=()):
        ws = {}

        def need(t, war=False):
            if t is None:
                return
            if t[0] == "eng" and t[1] == eng and (eng == "pe" or war):
                return
            k, v = self._resolve(t)
            if self.waited[eng].get(k, 0) >= v:
                return
            if ws.get(k, 0) < v:
                ws[k] = v

        for b in reads:
            need(b.w)
        for b in writes:
            need(b.w)
            for t in b.r.values():
                need(t, war=True)
        for t in extra:
            need(t)
        for k, v in ws.items():
            self.waited[eng][k] = v
        return list(ws.items())

    @staticmethod
    def _addr(b, t):
        key = (t[0], t[1])
        old = b.r.get(key)
        if old is None or old[2] < t[2]:
            b.r[key] = t

    def op(self, eng, fn, reads=(), writes=(), sig=None):
        px = [b for b in reads if b.name.startswith("psum")]
        if px:
            reads = [b for b in reads if not b.name.startswith("psum")]
            writes = list(writes) + [b for b in px if b not in writes]
        waits = self._waits(eng, reads, writes)
        st = self.streams[eng]
        pos = len(st)
        if sig is None:
            sig = getattr(fn, "sig", eng != "pe")
        st.append({"fn": fn, "waits": waits, "sig": bool(sig), "dma": None})
        if sig:
            self.sigpos[eng].append(pos)
        self.lastc[eng] = pos
        t = ("eng", eng, pos)
        for b in reads:
            self._addr(b, t)
        for b in writes:
            b.w = t
            b.r = {}
        return t

    def dma(self, eng, fn, sem, reads=(), writes=(), extra=()):
        waits = self._waits(eng, reads, writes, extra=extra)
        self.dmacnt[sem] = self.dmacnt.get(sem, 0) + 16
        self.streams[eng].append({"fn": fn, "waits": waits, "sig": False, "dma": sem})
        t = ("dma", sem, self.dmacnt[sem])
        for b in reads:
            self._addr(b, t)
        for b in writes:
            b.w = t
            b.r = {}
        return t

    def barrier(self, engs=None):
        tickets = [("eng", e, self.lastc[e]) for e in self.ENGS if self.lastc[e] >= 0]
        tickets += [("dma", s, c) for s, c in self.dmacnt.items()]
        for e in (engs or self.ENGS):
            waits = self._waits(e, (), (), extra=tickets)
            self.streams[e].append({"fn": None, "waits": waits, "sig": False, "dma": None})

    def final_wait(self, eng, sems):
        waits = [(s, self.dmacnt[s]) for s in sems if s in self.dmacnt]
        self.streams[eng].append({"fn": None, "waits": waits, "sig": False, "dma": None})

    def emit(self):
        nc = self.nc
        with ExitStack() as es:
            sems = {}
            for k in list(self.ENGS) + sorted(self.dmacnt.keys()):
                sems[k] = es.enter_context(nc.semaphore("s_" + k))
            block = es.enter_context(nc.Block())

            def run(engname):
                def body(eng):
                    for it in self.streams[engname]:
                        for k, v in it["waits"]:
                            eng.wait_ge(sems[k], v)
                        if it["fn"] is None:
                            continue
                        ins = it["fn"](eng)
                        if it["dma"] is not None:
                            ins.then_inc(sems[it["dma"]], 16)
                        elif it["sig"]:
                            ins.then_inc(sems[engname], 1)
                return body

            block.tensor(run("pe"))
            block.scalar(run("act"))
            block.vector(run("dve"))
            block.gpsimd(run("pool"))
            block.sync(run("sp"))


def build_program(n_att=NH, n_rec=NH, phase_b=True, taps=()):
    nc = bass.Bass("TRN2", target_bir_lowering=False)

    def din(name, shape, dt=F32):
        return nc.dram_tensor(name, list(shape), dt, kind="ExternalInput").ap()

    x = din("x", [T, D])
    w_in = din("w_in", [128, 128, D])
    w_out = din("w_out", [8, 128, NCH * 512])
    nw = din("nw", [1, D])
    fw = din("fw", [1, D])
    lgt = din("lgt", [128, 32])
    rwt = din("rwt", [128, 16])
    cf = din("cf", [128, NCF])
    cb = din("cb", [128, NCB], BF16)
    out = nc.dram_tensor("out", [T, D], F32, kind="ExternalOutput").ap()
    wob = nc.dram_tensor("wob", [8, 128, NCH * 512], BF16, kind="Internal").ap()
    mixd = nc.dram_tensor("mixd", [16, 128, NCH * 128], BF16, kind="Internal").ap()
    tap_out = {}
    for name, shape, dt in taps:
        tap_out[name] = nc.dram_tensor("tap_" + name, list(shape), dt, kind="ExternalOutput").ap()

    es = ExitStack()
    with es:
        AR = 105000
        arena = es.enter_context(nc.sbuf_tensor("arena", [128, AR], BF16))
        ps = es.enter_context(nc.psum_tensor("ps", [128, 4096], F32))
        psb = ps[:, :].bitcast(BF16)

        def vb(off, n):
            assert off % 2 == 0 and off // 2 + n <= AR
            return arena[:, off // 2: off // 2 + n]

        def vf(off, n):
            assert off % 4 == 0 and off // 2 + 2 * n <= AR
            return arena[:, off // 2: off // 2 + 2 * n].bitcast(F32)

        def bank(i):
            return ps[:, i * 512:(i + 1) * 512]

        def bankb(i):
            return psb[:, i * 1024:(i + 1) * 1024]

        P = Prog(nc)
        PB = [Buf("psum%d" % i) for i in range(8)]

        XT = 0
        WS = 131072
        HR = WS + 3 * 8192
        CR = HR + 32768
        xT = vb(XT, NCH * T)
        xT3 = xT.rearrange("p (c t) -> p c t", c=NCH)
        wslot = [vb(WS + i * 8192, 4096) for i in range(3)]
        Bw = [Buf("w%d" % i) for i in range(3)]
        BxT = [Buf("xT%d" % i) for i in range(16)]

        o = CR
        cfs = vf(o, NCF); o += NCF * 4
        cbs = vb(o, NCB); o += NCB * 2
        ones_b = vb(o, 128); o += 256
        ones_f = vf(o, 128); o += 512
        lgs = vf(o, 32); o += 128
        lb = vf(o, 16); o += 64
        oml = vf(o, 16); o += 64
        noml = vf(o, 16); o += 64
        rws = vf(o, 16); o += 64
        lncb = vf(o, 1); o += 4
        epsb = vf(o, 1); o += 4
        oneb = vf(o, 1); o += 4
        ss = vf(o, 1); o += 4
        lnv = vf(o, 1); o += 4
        rstd1 = vf(o, 1); o += 4
        o += 8
        km_b = vb(o, 8); o += 16
        ksum = vf(o, 8); o += 32
        gp = vf(o, 8); o += 32
        m8 = vf(o, 8); o += 32
        sel = vf(o, 8); o += 32
        negm = vf(o, 8); o += 32
        gsm = [(gp, m8, sel, negm), tuple(vf(o + 32 * i_, 8) for i_ in range(4))]; o += 128
        dcs = vf(o, 32); o += 128
        Tst = vf(o, 128); o += 512
        Sbf = vb(o, 128); o += 256
        Sbf_b = vb(o, 128); o += 256
        scb = vb(o, 64); o += 128
        gset = [tuple(vf(o + 96 * i_ + 32 * k_, 8) for k_ in range(3)) for i_ in range(8)]; o += 768
        negmB = [vb(o + 256 * i_, 128) for i_ in range(8)]; o += 2048
        scbz = [vb(o, 64), vb(o + 128, 64)]; o += 256
        assert o <= AR * 2, o
        Bc = Buf("consts")
        ident_f = cfs[:, CF_ID:CF_ID + 128]
        tri01 = cfs[:, CF_TRI:CF_TRI + 64]
        scanm = cfs[:, CF_SCAN:CF_SCAN + 512]
        rope = cfs[:, CF_ROPE:CF_ROPE + T]
        ident_b = cbs[:, CB_ID:CB_ID + 128]
        causneg = cbs[:, CB_CAUS:CB_CAUS + 128]
        esel = cbs[:, CB_ESEL:CB_ESEL + 1024]

        class _F:
            def __init__(self, f, sig):
                self.f = f
                self.sig = sig

            def __call__(self, e):
                return self.f(e)

        def mm(out_, lhsT, rhs, start, stop):
            return _F(lambda e: e.matmul(out_, lhsT, rhs, start=start, stop=stop), bool(stop))

        def tr(out_, in_, idt):
            return _F(lambda e: e.transpose(out_, in_, idt), True)

        def act(out_, in_, func, scale=None, bias=None, accum=None):
            kw = {}
            if scale is not None:
                kw["scale"] = scale
            if bias is not None:
                kw["bias"] = bias
            if accum is not None:
                kw["accum_out"] = accum
            return lambda e: e.activation(out_, in_, func, **kw)

        def tt(out_, a, b, op):
            return lambda e: e.tensor_tensor(out_, a, b, op)

        def ts(out_, a, s1, s2, op0, op1=None):
            if op1 is None:
                return lambda e: e.tensor_scalar(out_, a, s1, None, op0)
            return lambda e: e.tensor_scalar(out_, a, s1, s2, op0, op1)

        def stt(out_, a, s, b, op0, op1):
            return lambda e: e.scalar_tensor_tensor(out_, a, s, b, op0, op1)

        def cp(out_, in_):
            return lambda e: e.tensor_copy(out_, in_)

        def mset(out_, v):
            return lambda e: e.memset(out_, v)

        def dmaf(out_, in_):
            return lambda e: e.dma_start(out=out_, in_=in_)

        def tap(name, ap, bufs):
            if name in tap_out:
                P.dma("sp", dmaf(tap_out[name], ap), "tap", reads=bufs)

        P.dma("sp", dmaf(cfs, cf), "ldc0", writes=[Bc])
        P.dma("sp", dmaf(cbs, cb), "ldc1", writes=[Bc])
        P.dma("sp", dmaf(lgs, lgt), "ldc2", writes=[Bc])
        P.dma("sp", dmaf(rws, rwt), "ldc3", writes=[Bc])
        P.op("pool", mset(ones_b, 1.0), writes=[Bc])
        P.op("pool", mset(ones_f, 1.0), writes=[Bc])
        P.op("pool", mset(lncb, float(np.log(SC))), writes=[Bc])
        P.op("pool", mset(epsb, EPS), writes=[Bc])
        P.op("pool", mset(oneb, 1.0), writes=[Bc])
        Bz = Buf("zeropad")
        for i_ in range(8):
            P.op("pool", mset(negmB[i_], 0.0), writes=[Bz])
        Bg8 = [Buf("g8_%d" % i_) for i_ in range(8)]
        P.op("pool", mset(scbz[0], 0.0), writes=[Bz])
        P.op("pool", mset(scbz[1], 0.0), writes=[Bz])
        P.op("dve", tt(lb, lgs[:, 16:32], lgs[:, 0:16], ALU.subtract), reads=[Bc], writes=[Bc])
        P.op("act", act(lb, lb, AF.Exp), reads=[Bc], writes=[Bc])
        P.op("dve", ts(lb, lb, 1.0, None, ALU.add), reads=[Bc], writes=[Bc])
        P.op("dve", lambda e: e.reciprocal(lb, lb), reads=[Bc], writes=[Bc])
        P.op("dve", ts(oml, lb, -1.0, 1.0, ALU.mult, ALU.add), reads=[Bc], writes=[Bc])
        P.op("dve", ts(noml, oml, -1.0, None, ALU.mult), reads=[Bc], writes=[Bc])

        xs2 = [vf(WS, D), vf(WS + 16384, D)]
        u_b = vb(WS + 32768, D)
        nwb = vf(WS + 40960, D)
        Bxs2, Bu, Bnw, Bst = [Buf("xs0"), Buf("xs1")], Buf("u"), Buf("nwb"), Buf("stat")
        P.dma("sp", dmaf(nwb, nw[0:1, :].partition_broadcast(128)), "ldc4", writes=[Bnw])
        ev = 0
        for tt_ in range(16):
            xs, Bxs = xs2[tt_ % 2], Bxs2[tt_ % 2]
            P.dma("sp", dmaf(xs, x[tt_ * 128:(tt_ + 1) * 128, :]), "ldx%d" % (tt_ % 2), writes=[Bxs])
            P.op("act", act(u_b, xs, AF.Square, accum=ss), reads=[Bxs], writes=[Bu, Bst])
            P.op("act", act(lnv, ss, AF.Ln, scale=1.0 / D, bias=epsb), reads=[Bst, Bc], writes=[Bst])
            P.op("act", act(rstd1, lnv, AF.Exp, scale=-0.5), reads=[Bst], writes=[Bst])
            P.op("dve", stt(u_b, xs, rstd1, nwb, ALU.mult, ALU.mult), reads=[Bxs, Bst, Bnw], writes=[Bu])
            for c0 in range(0, NCH, 8):
                bk = 4 + (ev % 4)
                for c in range(c0, c0 + 8):
                    P.op("pe", tr(bankb(bk)[:, (c - c0) * 128:(c - c0 + 1) * 128],
                                  u_b[:, c * 128:(c + 1) * 128], ident_b),
                         reads=[Bu, Bc], writes=[PB[bk]])
                dst = xT3[:, c0:c0 + 8, tt_ * 128:(tt_ + 1) * 128]
                src = bankb(bk).rearrange("p (c t) -> p c t", c=8)
                if ev % 2 == 0:
                    P.op("act", lambda e, d=dst, s=src: e.copy(d, s), reads=[PB[bk]], writes=[BxT[tt_]])
                else:
                    P.op("dve", cp(dst, src), reads=[PB[bk]], writes=[BxT[tt_]])
                ev += 1
        tap("xT", xT3[:, 0, :], BxT)
        P.barrier()

        sched = []
        for h in range(n_att):
            sched += [h, 16 + h, 32 + h, 48 + h]
        for h in range(n_rec):
            sched += [64 + h, 80 + h, 96 + h, 112 + h]
        NWS = 2
        wstate = {"next": 0, "slot_of": {}}

        def prefetch(upto):
            while wstate["next"] < min(upto, len(sched)):
                i = wstate["next"]
                s = i % NWS
                P.dma("pool", dmaf(wslot[s], w_in[sched[i]]), "ldw%d" % s, writes=[Bw[s]])
                wstate["slot_of"][i] = s
                wstate["next"] += 1

        widx = {"i": 0, "bank": 0}

        def inproj_gen(evac, unit=8, state=None, banks=(0, 1)):
            i = widx["i"]
            widx["i"] += 1
            prefetch(i + NWS)
            s = wstate["slot_of"][i]
            for g in range(4):
                bk = banks[widx["bank"] % len(banks)]
                widx["bank"] += 1
                for c in range(NCH):
                    rd = ([Bw[s]] + BxT[4 * g:4 * g + 4]) if c == 0 else []
                    P.op("pe", mm(bank(bk), wslot[s][:, c * 128:(c + 1) * 128],
                                  xT3[:, c, g * 512:(g + 1) * 512], c == 0, c == NCH - 1),
                         reads=rd, writes=[PB[bk]])
                    if c == NCH - 1:
                        tl = ("eng", "pe", len(P.streams["pe"]) - 1)
                        for b_ in BxT[4 * g:4 * g + 4]:
                            Prog._addr(b_, tl)
                        if g == 3:
                            Prog._addr(Bw[s], tl)
                    if (c + 1) % unit == 0 and c != NCH - 1:
                        yield
                evac(g, bk)
                if state is not None:
                    state["done"] = g + 1
                yield

        def run(gen):
            for _ in gen:
                pass

        def step(gen, n=1):
            for _ in range(n):
                try:
                    next(gen)
                except StopIteration:
                    return False
            return True

        AUX = WS + 2 * 8192
        qT = vb(HR + 0, T)
        kT = vb(HR + 4096, T)
        vTm = vb(HR + 8192, T)
        v_tm = vb(HR + 12288, T)
        sg = vb(HR + 16384, T)
        PT = [vb(HR + 20480 + i * 1024, 512) for i in range(2)] + [vb(HR + 22528, 512)]
        tA = vf(HR + 22528, 512)
        tB = vf(HR + 24576, 512)
        rden = vf(HR + 26624, 512)
        tmpo = vf(HR + 28672, 512)
        negmT = vb(AUX, T)
        qs = vf(AUX, T)
        k_tm = vb(HR + 20480, T)
        tmp1 = vf(HR + 24576, 512)
        tmp2 = vf(HR + 26624, 512)
        tmp3 = vf(HR + 28672, 512)
        tmp4 = vf(HR + 30720, 512)
        BqT, BkT, BvT, Bvtm, Bsg = Buf("qT"), Buf("kT"), Buf("vT"), Buf("vtm"), Buf("sg")
        Bmix = Buf("mix")
        BPT = [Buf("PT0"), Buf("PT1"), None]
        BtA, BtB = Buf("tA"), Buf("tB")
        BPT[2] = BtA
        Bng = Buf("negmT")
        Brd, Bto, Bsm = Buf("rden"), Buf("tmpo"), Buf("small")
        Bgs = [Buf("gs0"), Buf("gs1")]

        def rope_evac(dstT, Bdst):
            def f(g, bk):
                tc_ = slice(g * 512, (g + 1) * 512)
                Pg = bank(bk)
                P.op("act", lambda e, d=dstT[:, tc_], s=Pg: e.copy(d, s), reads=[PB[bk]], writes=[Bdst])
                P.op("dve", tt(tA[0:16, :], Pg[0:16, :], rope[0:16, tc_], ALU.mult), reads=[PB[bk], Bc], writes=[BtA])
                P.op("dve", tt(tB[0:16, :], Pg[64:80, :], rope[32:48, tc_], ALU.mult), reads=[PB[bk], Bc], writes=[BtB])
                P.op("dve", tt(tA[64:80, :], Pg[64:80, :], rope[64:80, tc_], ALU.mult), reads=[PB[bk], Bc], writes=[BtA])
                P.op("dve", tt(tB[64:80, :], Pg[0:16, :], rope[96:112, tc_], ALU.mult), reads=[PB[bk], Bc], writes=[BtB])
                P.op("pool", tt(dstT[0:16, tc_], tA[0:16, :], tB[0:16, :], ALU.subtract), reads=[BtA, BtB], writes=[Bdst])
                P.op("pool", tt(dstT[64:80, tc_], tA[64:80, :], tB[64:80, :], ALU.add), reads=[BtA, BtB], writes=[Bdst])
            return f

        def copy_evac(dstT, Bdst):
            def f(g, bk):
                d = dstT[:, g * 512:(g + 1) * 512]
                if g % 2 == 0:
                    P.op("act", lambda e, d=d, s=bank(bk): e.copy(d, s), reads=[PB[bk]], writes=[Bdst])
                else:
                    P.op("dve", cp(d, bank(bk)), reads=[PB[bk]], writes=[Bdst])
            return f

        deferred = []

        def flush_deferred():
            while deferred:
                deferred.pop(0)()

        def silu_evac(dstT, Bdst, tmpE, BtE, defer=False):
            def f(g, bk):
                d = dstT[:, g * 512:(g + 1) * 512]
                P.op("act", act(tmpE, bank(bk), AF.Exp, scale=-1.0), reads=[PB[bk]], writes=[BtE])
                P.op("act", act(tmpE, tmpE, AF.Ln, bias=oneb), reads=[BtE, Bc], writes=[BtE])
                P.op("act", act(tmpE, tmpE, AF.Exp, scale=-1.0), reads=[BtE], writes=[BtE])

                def fin():
                    P.op("dve", tt(d, bank(bk), tmpE, ALU.mult), reads=[PB[bk], BtE], writes=[Bdst])
                if defer:
                    deferred.append(fin)
                else:
                    fin()
            return f

        def to_token_major(srcT, Bsrc, dst_tm, Bdst):
            for half in range(2):
                bk = 2 + half
                for j in range(8):
                    t_ = half * 8 + j
                    P.op("pe", tr(bankb(bk)[:, j * 128:(j + 1) * 128], srcT[:, t_ * 128:(t_ + 1) * 128], ident_b),
                         reads=[Bsrc, Bc], writes=[PB[bk]])
                d = dst_tm[:, half * 1024:(half + 1) * 1024]
                if half == 0:
                    P.op("act", lambda e, d=d, s=bankb(bk): e.copy(d, s), reads=[PB[bk]], writes=[Bdst])
                else:
                    P.op("dve", cp(d, bankb(bk)), reads=[PB[bk]], writes=[Bdst])

        def gate_chain():
            for t_ in range(8, 16):
                P.op("pe", mm(bank(2)[:, (t_ - 8) * 8:(t_ - 8) * 8 + 8], qT[:, t_ * 128:(t_ + 1) * 128], km_b, True, True),
                     reads=[BqT, Bsm], writes=[PB[2]])
            for t_ in range(8, 16):
                i_ = t_ - 8
                b = t_ // 2
                gp_, m8_, sel_ = gset[i_]
                nb_ = negmB[i_]
                gsrc = bank(2)[:, i_ * 8:i_ * 8 + b]
                P.op("pool", mset(gp_, -1e30), reads=[Bz], writes=[Bg8[i_]])
                P.op("pool", mset(nb_[:, 0:8], 0.0), reads=[Bz], writes=[Bg8[i_]])
                P.op("dve", cp(gp_[:, 0:b], gsrc), reads=[PB[2]], writes=[Bg8[i_]])
                P.op("dve", lambda e, m=m8_, g_=gp_: e.max(m, g_), reads=[Bg8[i_]], writes=[Bg8[i_]])
                P.op("dve", ts(sel_[:, 0:b], gp_[:, 0:b], m8_[:, 2:3], None, ALU.is_ge), reads=[Bg8[i_]], writes=[Bg8[i_]])
                P.op("dve", ts(nb_[:, 0:b], sel_[:, 0:b], -1.0, -NEG, ALU.add, ALU.mult), reads=[Bg8[i_]], writes=[Bg8[i_]])

        def gate_finish():
            for i_ in range(8):
                P.op("pe", tr(bankb(3)[:, i_ * 128:(i_ + 1) * 128], negmB[i_], ident_b),
                     reads=[Bg8[i_], Bc], writes=[PB[3]])
            P.op("act", lambda e: e.copy(negmT[:, 1024:2048], bankb(3)), reads=[PB[3]], writes=[Bng])

        def att_main_gen(h):
            for g in range(4):
                kts = list(range(4 * g + 4))

                bO, bD = (6, 7) if g % 2 == 0 else (2, 3)

                def pv(kt, slot, col0, first, last, bO=bO, bD=bD):
                    w = 512 - col0
                    P.op("pe", mm(bank(bO)[:, col0:512], v_tm[:, kt * 128:(kt + 1) * 128], PT[slot][:, 0:w], first, last),
                         reads=[Bvtm, BPT[slot]], writes=[PB[bO]])
                    P.op("pe", mm(bank(bD)[:, col0:512], ones_b, PT[slot][:, 0:w], first, last),
                         reads=[Bc, BPT[slot]], writes=[PB[bD]])

                pend = []
                for i, kt in enumerate(kts):
                    slot = i % 2
                    bk = 4 + slot
                    col0 = max(0, kt - 4 * g) * 128
                    w = 512 - col0
                    diag = kt >= 4 * g
                    j = kt // 2
                    need_mask = (g >= 2) and (j < 2 * g + 1)
                    P.op("pe", mm(bank(bk)[:, 0:w], kT[:, kt * 128:(kt + 1) * 128],
                                  qT[:, g * 512 + col0:(g + 1) * 512], True, not (diag or need_mask)),
                         reads=[BkT, BqT], writes=[PB[bk]])
                    if diag:
                        P.op("pe", mm(bank(bk)[:, 0:128], ident_b, causneg, False, not need_mask),
                             reads=[Bc], writes=[PB[bk]])
                    if need_mask:
                        P.op("pe", mm(bank(bk)[:, 0:w], esel[:, j * 128:(j + 1) * 128],
                                      negmT[:, g * 512 + col0:(g + 1) * 512], False, True),
                             reads=[Bc, Bng], writes=[PB[bk]])
                    P.op("act", act(PT[slot][:, 0:w], bank(bk)[:, 0:w], AF.Exp, scale=SC),
                         reads=[PB[bk]], writes=[BPT[slot]])
                    pend.append((kt, slot, col0, i == 0, i == len(kts) - 1))
                    if len(pend) > 1:
                        pv(*pend.pop(0))
                    yield ("step", g)
                while pend:
                    pv(*pend.pop(0))
                yield ("need", g)
                gc = slice(g * 512, (g + 1) * 512)
                P.op("act", act(rden, bank(bD), AF.Ln), reads=[PB[bD]], writes=[Brd])
                P.op("act", act(rden, rden, AF.Exp, scale=-1.0), reads=[Brd], writes=[Brd])
                P.op("dve", tt(tmpo, bank(bO), rden, ALU.mult), reads=[PB[bO], Brd], writes=[Bto])
                P.op("pool", tt(vTm[:, gc], tmpo, sg[:, gc], ALU.mult), reads=[Bto, Bsg], writes=[Bmix])
            P.dma("sp", dmaf(mixd[:, :, h * 128:(h + 1) * 128].rearrange("a p t -> p a t"), vTm.rearrange("p (a t) -> p a t", a=16)), "stm", reads=[Bmix])

        def drive(main, ip, ipstate, per_step):
            alive = True
            for item in main:
                kind, g = item
                if kind == "need":
                    while alive and ipstate["done"] <= g:
                        alive = step(ip)
                    flush_deferred()
                else:
                    flush_deferred()
                    if alive:
                        alive = step(ip, per_step[g])
            while alive:
                alive = step(ip)
            flush_deferred()

        Bwob = [Buf("wob%d" % i) for i in range(8)]
        for h in range(n_att):
            if h < 8:
                P.dma("pool", dmaf(wob[h], w_out[h]), "cvt%d" % h, writes=[Bwob[h]])
            run(inproj_gen(rope_evac(qT, BqT), banks=(0, 1, 2, 3)))
            run(inproj_gen(rope_evac(kT, BkT), banks=(0, 1, 2, 3)))
            P.op("dve", lambda e: e.tensor_reduce(ksum, kT.rearrange("p (b t) -> p b t", b=8), AX.X, ALU.add),
                 reads=[BkT], writes=[Bsm])
            P.op("dve", ts(km_b, ksum, 1.0 / 256.0, None, ALU.mult), reads=[Bsm], writes=[Bsm])
            ipv = inproj_gen(copy_evac(vTm, Bmix), unit=8)
            step(ipv, 10)
            gate_chain()
            run(ipv)
            to_token_major(vTm, Bmix, v_tm, Bvtm)
            gate_finish()
            st = {"done": 0}
            ipg = inproj_gen(silu_evac(sg, Bsg, tB, BtB), unit=1, state=st, banks=(0, 1))
            drive(att_main_gen(h), ipg, st, {0: 8, 1: 4, 2: 3, 3: 2})
            if h == 0:
                tap("qT", qT, [BqT]); tap("kT", kT, [BkT]); tap("vtm", v_tm, [Bvtm]); tap("sg", sg, [Bsg])
                tap("mix0", vTm, [Bmix])

        P.barrier()

        BqpT, BkpT, Bktm, Bqs = Buf("qpT"), Buf("kpT"), Buf("ktm"), Buf("qs")
        Bt1, Bt2, Bt3, Bt4 = Buf("tmp1"), Buf("tmp2"), Buf("tmp3"), Buf("tmp4")
        Bdc, BT = Buf("dc"), Buf("T")
        BS = [Buf("Sbf0"), Buf("Sbf1")]
        Bsc = [Buf("scb0"), Buf("scb1")]
        qpT, kpT = qT, kT

        e2buf = vf(HR + 20480, 512)
        ebuf = [tmp1, e2buf]
        k_tmH = [k_tm, vb(AUX, T)]
        Be = [Bt1, Bktm]

        def rf_evac(h):
            hc = slice(h, h + 1)

            def head(g, bk):
                P.op("act", act(ebuf[g % 2], bank(bk), AF.Exp), reads=[PB[bk]], writes=[Be[g % 2]])

            def tail(g):
                gc = slice(g * 512, (g + 1) * 512)
                eb, Beb = ebuf[g % 2], Be[g % 2]
                P.op("act", act(tmp2, eb, AF.Ln, bias=lb[:, hc]), reads=[Beb, Bc], writes=[Bt2])
                P.op("act", act(tmp4, eb, AF.Ln, bias=oneb), reads=[Beb, Bc], writes=[Bt4])
                P.op("dve", tt(tmp2, tmp2, tmp4, ALU.subtract), reads=[Bt2, Bt4], writes=[Bt2])
                P.op("dve", lambda e: e.tensor_tensor_scan(tmp3, scanm, tmp2, 0.0, ALU.mult, ALU.add),
                     reads=[Bt2, Bc], writes=[Bt3])
                P.op("act", act(eb, tmp4, AF.Exp, scale=-1.0), reads=[Bt4], writes=[Beb])
                P.op("act", act(tmp2, tmp3, AF.Exp, scale=-1.0), reads=[Bt3], writes=[Bt2])
                P.op("dve", stt(kpT[:, gc], eb, oml[:, hc], tmp2, ALU.mult, ALU.mult),
                     reads=[Beb, Bt2, Bc], writes=[BkpT])
                P.op("act", act(dcs[:, g * 8:(g + 1) * 8], tmp3.rearrange("p (c j) -> p c j", j=64)[:, :, 63], AF.Exp),
                     reads=[Bt3], writes=[Bdc])
                P.op("act", act(tmp4, tmp3, AF.Exp, bias=lncb), reads=[Bt3, Bc], writes=[Bt4])
                P.op("dve", tt(qpT[:, gc], qs[:, gc], tmp4, ALU.mult), reads=[Bqs, Bt4], writes=[BqpT])

            def f(g, bk):
                head(g, bk)
                if g > 0:
                    tail(g - 1)
                if g == 3:
                    tail(3)
            return f

        def rec_main_gen(h):
            hc = slice(h, h + 1)

            def front(n):
                t_ = n // 2
                cs = slice(n * 64, (n + 1) * 64)
                tcs = slice(t_ * 128, (t_ + 1) * 128)
                bk = 4 if n % 2 == 0 else 6
                P.op("pe", mm(bank(bk)[:, 0:64], kpT[:, tcs], qpT[:, cs], True, True),
                     reads=[BkpT, BqpT], writes=[PB[bk]])
                P.op("pe", mm(bank(bk)[:, 128:256], k_tmH[n % 2][:, tcs], v_tm[:, tcs], True, True),
                     reads=[Bktm, Bqs, Bvtm], writes=[PB[bk]])

            def fin_a(g):
                bo = 2 + (g % 2)
                P.op("act", act(tmp1, bank(bo), AF.Square), reads=[PB[bo]], writes=[Bt1])

            def fin_a2(g):
                bn = 5 if g % 2 == 0 else 7
                P.op("pe", mm(bank(bn), ones_f, tmp1, True, True), reads=[Bc, Bt1], writes=[PB[bn]])

            def fin_b(g):
                bn = 5 if g % 2 == 0 else 7
                P.op("act", act(tmp2, bank(bn), AF.Ln, scale=1.0 / 128.0, bias=epsb), reads=[PB[bn], Bc], writes=[Bt2])
                P.op("act", act(tmp2, tmp2, AF.Exp, scale=-0.5), reads=[Bt2], writes=[Bt2])

            def fin_c(g):
                bo = 2 + (g % 2)
                gc = slice(g * 512, (g + 1) * 512)
                P.op("dve", stt(tmp3, bank(bo), rws[:, hc], tmp2, ALU.mult, ALU.mult),
                     reads=[PB[bo], Bc, Bt2], writes=[Bt3])
                P.op("pool", tt(vTm[:, gc], tmp3, sg[:, gc], ALU.mult), reads=[Bt3, Bsg], writes=[Bmix])

            front(0)
            for n in range(32):
                t_ = n // 2
                r0 = (n % 2) * 64
                par = n % 2
                cs = slice(n * 64, (n + 1) * 64)
                g = n // 8
                oc = slice((n % 8) * 64, (n % 8 + 1) * 64)
                bk = 4 if par == 0 else 6
                bo = 2 + (g % 2)
                P.op("dve", tt(scbz[par][r0:r0 + 64, :], bank(bk)[r0:r0 + 64, 0:64], tri01[r0:r0 + 64, :], ALU.mult),
                     reads=[PB[bk], Bc, Bz], writes=[Bsc[par]])
                if n == 0:
                    P.op("dve", cp(Tst, bank(bk)[:, 128:256]), reads=[PB[bk]], writes=[BT])
                else:
                    P.op("dve", stt(Tst, Tst, dcs[:, n - 1:n], bank(bk)[:, 128:256], ALU.mult, ALU.add),
                         reads=[BT, Bdc, PB[bk]], writes=[BT])
                if n < 31:
                    P.op("dve", ts(Sb2[par], Tst, dcs[:, n:n + 1], None, ALU.mult), reads=[BT, Bdc], writes=[BS[par]])
                    front(n + 1)
                P.op("pe", mm(bank(bo)[:, oc], v_tm[:, t_ * 128:(t_ + 1) * 128], scbz[par], True, n == 0),
                     reads=[Bvtm, Bsc[par]], writes=[PB[bo]])
                if n > 0:
                    P.op("pe", mm(bank(bo)[:, oc], Sb2[1 - par], qpT[:, cs], False, True),
                         reads=[BS[1 - par], BqpT], writes=[PB[bo]])
                if g > 0 and n % 8 == 1:
                    fin_a2(g - 1)
                if g > 0 and n % 8 == 3:
                    fin_b(g - 1)
                yield ("step", g)
                if g > 0 and n % 8 == 4:
                    yield ("need", g - 1)
                    fin_c(g - 1)
                if n % 8 == 7:
                    fin_a(g)
            fin_a2(3)
            fin_b(3)
            yield ("need", 3)
            fin_c(3)
            P.dma("sp", dmaf(mixd[:, :, (16 + h) * 128:(17 + h) * 128].rearrange("a p t -> p a t"),
                             vTm.rearrange("p (a t) -> p a t", a=16)), "stm", reads=[Bmix])

        Sb2 = [Sbf, Sbf_b]
        for h in range(n_rec):
            run(inproj_gen(silu_evac(qs, Bqs, tmp4, Bt4), banks=(0, 1, 2, 3)))
            run(inproj_gen(rf_evac(h), banks=(0, 1, 2, 3)))
            run(inproj_gen(copy_evac(vTm, Bmix), banks=(0, 1, 2, 3)))
            to_token_major(vTm, Bmix, v_tm, Bvtm)
            P.op("pool", mset(k_tmH[0][64:128, :], 0.0), writes=[Bktm])
            P.op("pool", mset(k_tmH[1][0:64, :], 0.0), writes=[Bqs])
            for half in range(2):
                bk = 2 + half
                for j in range(8):
                    t_ = half * 8 + j
                    P.op("pe", tr(bankb(bk)[:, j * 128:(j + 1) * 128], kpT[:, t_ * 128:(t_ + 1) * 128], ident_b),
                         reads=[BkpT, Bc], writes=[PB[bk]])
                hs_ = slice(half * 1024, (half + 1) * 1024)
                P.op("act", lambda e, d=k_tmH[0][0:64, hs_], s_=bankb(bk)[0:64, :]: e.copy(d, s_),
                     reads=[PB[bk]], writes=[Bktm])
                P.op("dve", cp(k_tmH[1][64:128, hs_], bankb(bk)[64:128, :]), reads=[PB[bk]], writes=[Bqs])
            st = {"done": 0}
            ipg = inproj_gen(silu_evac(sg, Bsg, tmp4, Bt4, defer=True), unit=1, state=st)
            drive(rec_main_gen(h), ipg, st, {0: 4, 1: 4, 2: 4, 3: 4})
            if h == 0:
                tap("qpT", qpT, [BqpT]); tap("kpT", kpT, [BkpT]); tap("dc", dcs, [Bdc])
                tap("mix16", vTm, [Bmix])

        if phase_b:
            mixq = vb(0, NCH * 512)
            wo = [vb(32768 + i * 32768, NCH * 512) for i in range(2)]
            xh = vf(98304, 4 * D)
            fwb = vf(163840, D)
            junk = vb(180224, D)
            o = 188416
            ssB = [vf(o + 16 * j, 1) for j in range(4)]
            lnB = [vf(o + 16 * j + 4, 1) for j in range(4)]
            rsB = [vf(o + 16 * j + 8, 1) for j in range(4)]
            epB = vf(o + 64, 1)
            Bmq = [Buf("mixq%d" % j) for j in range(4)]
            Bxs = [[Buf("xh%d_%d" % (j, n_)) for n_ in range(8)] for j in range(4)]
            Bfw, Bjk, Bep = Buf("fwb"), Buf("junk"), Buf("epB")
            BstB = [Buf("statB%d" % j) for j in range(4)]
            Bwo = [Buf("wo0"), Buf("wo1")]
            stm_t = [("dma", "stm", P.dmacnt["stm"])] if "stm" in P.dmacnt else []
            for j in range(4):
                P.dma("pool", dmaf(mixq[:, j * 4096:(j + 1) * 4096], mixd[j]), "ldm%d" % j,
                      writes=[Bmq[j]] + (BxT if j == 0 else []), extra=stm_t)
            wsched = [(q4, ng) for q4 in range(4) for ng in range(8)]

            def load_wo(i):
                s_ = i % 2
                ng_ = wsched[i][1]
                P.dma("act", dmaf(wo[s_], wob[ng_]), "ldo%d" % s_, reads=[Bwob[ng_]],
                      writes=[Bwo[s_]] + (BxT if i < 2 else []))

            load_wo(0)
            load_wo(1)
            P.barrier(engs=("act", "dve", "pool", "sp"))
            P.dma("sp", dmaf(fwb, fw[0:1, :].partition_broadcast(128)), "ldc5", writes=[Bfw])
            P.op("pool", mset(epB, EPS), writes=[Bep])
            def load_x(q4_):
                for n_ in range(8):
                    for j_ in range(4):
                        r_ = q4_ * 512 + j_ * 128
                        P.dma("sp", dmaf(xh[:, j_ * D + n_ * 512:j_ * D + (n_ + 1) * 512],
                                         x[r_:r_ + 128, n_ * 512:(n_ + 1) * 512]),
                              "lx%d_%d" % (j_, n_), writes=[Bxs[j_][n_]])

            load_x(0)
            for i, (q4, ng) in enumerate(wsched):
                s = i % 2
                wo3 = wo[s].rearrange("p (c n) -> p c n", c=NCH)
                for j in range(4):
                    bk = (ng * 4 + j) % 8
                    for c in range(NCH):
                        rd = [Bmq[j], Bwo[s]] if c == 0 else []
                        P.op("pe", mm(bank(bk), mixq[:, j * 4096 + c * 128:j * 4096 + (c + 1) * 128], wo3[:, c, :],
                                      c == 0, c == NCH - 1),
                             reads=rd, writes=[PB[bk]])
                    t = ("eng", "pe", len(P.streams["pe"]) - 1)
                    Prog._addr(Bwo[s], t)
                    Prog._addr(Bmq[j], t)
                    hs = xh[:, j * D + ng * 512: j * D + (ng + 1) * 512]
                    P.op("dve", tt(hs, bank(bk), hs, ALU.add), reads=[PB[bk]], writes=[Bxs[j][ng]])
                    if ng == 7:
                        if q4 < 3:
                            P.dma("pool", dmaf(mixq[:, j * 4096:(j + 1) * 4096], mixd[(q4 + 1) * 4 + j]), "ldm%d" % j,
                                  writes=[Bmq[j]])
                        r = q4 * 512 + j * 128
                        hj = xh[:, j * D:(j + 1) * D]
                        P.op("act", act(junk, hj, AF.Square, accum=ssB[j]), reads=Bxs[j], writes=[Bjk, BstB[j]])
                        P.op("act", act(lnB[j], ssB[j], AF.Ln, scale=1.0 / D, bias=epB), reads=[BstB[j], Bep], writes=[BstB[j]])
                        P.op("act", act(rsB[j], lnB[j], AF.Exp, scale=-0.5), reads=[BstB[j]], writes=[BstB[j]])
                        P.op("dve", stt(hj, hj, rsB[j], fwb, ALU.mult, ALU.mult), reads=[BstB[j], Bfw], writes=Bxs[j])
                        P.dma("sp", dmaf(out[r:r + 128, :], hj), "sto", reads=Bxs[j])
                if i + 2 < len(wsched):
                    load_wo(i + 2)
                if ng == 7 and q4 < 3:
                    load_x(q4 + 1)
        P.final_wait("sp", ["sto", "tap", "stm"])
        P.emit()
    return nc


def _constants():
    cf = np.zeros((128, NCF), np.float32)
    cf[:, CF_ID:CF_ID + 128] = np.eye(128, dtype=np.float32)
    s = np.arange(128)[:, None] % 64
    c = np.arange(64)[None, :]
    cf[:, CF_TRI:CF_TRI + 64] = (c >= s).astype(np.float32)
    sm = np.ones(512, np.float32)
    sm[::64] = 0.0
    cf[:, CF_SCAN:CF_SCAN + 512] = sm[None, :]
    half = 16
    inv_freq = np.power(np.float32(500000.0), -np.arange(half, dtype=np.float32) / np.float32(half)).astype(np.float32)
    ang = (np.arange(T, dtype=np.float32)[:, None] * inv_freq[None, :]).astype(np.float32)
    cos = np.cos(ang.astype(np.float64)).astype(np.float32).T
    sin = np.sin(ang.astype(np.float64)).astype(np.float32).T
    cf[0:16, CF_ROPE:CF_ROPE + T] = cos
    cf[32:48, CF_ROPE:CF_ROPE + T] = sin
    cf[64:80, CF_ROPE:CF_ROPE + T] = cos
    cf[96:112, CF_ROPE:CF_ROPE + T] = sin
    cb = np.zeros((128, NCB), np.float32)
    cb[:, CB_ID:CB_ID + 128] = np.eye(128)
    tk = np.arange(128)[:, None]
    tq = np.arange(128)[None, :]
    cb[:, CB_CAUS:CB_CAUS + 128] = np.where(tk > tq, NEG, 0.0)
    for j in range(8):
        cb[j, CB_ESEL + j * 128:CB_ESEL + (j + 1) * 128] = 1.0
    return cf, cb.astype(ml_dtypes.bfloat16)


_PERM = np.array([(2 * b1 + b0) * 16 + i for b0 in range(2) for b1 in range(4) for i in range(16)])


def _layout_inputs(inputs):
    w_in = np.asarray(inputs["w_in"])[0]
    w4 = w_in.reshape(NCH, 128, 128, 128)
    w_l = np.ascontiguousarray(w4.transpose(2, 1, 0, 3))
    w_l[0:32] = w_l[0:32][:, :, :, _PERM]
    w_l = w_l.reshape(128, 128, NCH * 128)
    w_out = np.asarray(inputs["w_out"])[0]
    wo = np.ascontiguousarray(w_out.reshape(NCH, 128, 8, 512).transpose(2, 1, 0, 3)).reshape(8, 128, NCH * 512)
    lg = np.asarray(inputs["rec_lower_bound_logits"])
    lgt = np.ascontiguousarray(lg.reshape(2, 16, 128).transpose(2, 0, 1)).reshape(128, 32)
    rw = np.asarray(inputs["rec_out_norm_w"])[0]
    rwt = np.ascontiguousarray(rw.reshape(16, 128).T)
    nw = np.asarray(inputs["norm_w"]).reshape(1, D)
    fw = np.asarray(inputs["final_norm_w"]).reshape(1, D)
    cf, cb = _constants()
    shared = {"w_in": w_l, "w_out": wo, "nw": nw, "fw": fw, "lgt": lgt, "rwt": rwt, "cf": cf, "cb": cb}
    return shared


def kernel(**inputs):
    x = np.asarray(inputs["x"])
    shared = _layout_inputs(inputs)
    nc = build_program()
    in_maps = []
    for b in range(8):
        m = dict(shared)
        m["x"] = np.ascontiguousarray(x[b])
        in_maps.append(m)
    res = run_bass_kernel_spmd(nc, in_maps, core_ids=list(range(8)))
    return np.stack([np.asarray(r["out"]) for r in res.results], axis=0).astype(np.float32)
```
